# Optimizing a Trainium2 kernel written in Bass

```python
import jax, jax.numpy as jnp
from jax import lax
import numpy as np

D_MODEL = 2048
BATCH = 1
SEQ = 8192
DEPTH = 2

PLE_DIM = 256
NORM_EPS = 1e-6
CONV_WIDTH = 4
N_BRANCH = 3
MIX_WIDTH = D_MODEL // 2
D_FF = 4 * D_MODEL

DN_HEAD_DIM = 128
DN_HEADS = MIX_WIDTH // DN_HEAD_DIM
DN_CHUNK = 64

SSM_HEAD_DIM = 64
SSM_HEADS = MIX_WIDTH // SSM_HEAD_DIM
SSM_GROUPS = 2
SSM_STATE = 128
SSM_CHUNK = 64

GLA_HEADS = 4
GLA_K_WIDTH = MIX_WIDTH // 2
GLA_K_DIM = GLA_K_WIDTH // GLA_HEADS
GLA_V_DIM = MIX_WIDTH // GLA_HEADS
GLA_GATE_RANK = 16
GLA_GATE_TEMP = 16.0
GLA_CHUNK = 16

IN_SPLITS = (
    MIX_WIDTH, MIX_WIDTH, MIX_WIDTH, DN_HEADS, DN_HEADS, MIX_WIDTH,
    MIX_WIDTH, MIX_WIDTH, SSM_GROUPS * SSM_STATE, SSM_GROUPS * SSM_STATE, SSM_HEADS,
    GLA_K_WIDTH, GLA_K_WIDTH, MIX_WIDTH, GLA_GATE_RANK, MIX_WIDTH,
    N_BRANCH * D_MODEL,
)
IN_TOTAL = sum(IN_SPLITS)

kernel_name = "hybrid_deltanet_ssd_gla_block"


def rmsnorm(x, gain):
    xf = x.astype(jnp.float32)
    y = xf * lax.rsqrt(jnp.mean(xf * xf, axis=-1, keepdims=True) + NORM_EPS)
    return (y * gain.astype(jnp.float32)).astype(x.dtype)


def l2norm(x):
    xf = x.astype(jnp.float32)
    return xf * lax.rsqrt(jnp.sum(xf * xf, axis=-1, keepdims=True) + NORM_EPS)


def causal_depthwise_conv(x, w, b=None):
    K, C = w.shape
    y = lax.conv_general_dilated(x, w[:, None, :].astype(x.dtype), window_strides=(1,),
                                 padding=[(K - 1, 0)], dimension_numbers=('NWC', 'WIO', 'NWC'),
                                 feature_group_count=C)
    if b is not None:
        y = y + b.astype(y.dtype)
    return y


def _to_chunks(t, chunk):
    Bsz, T, H = t.shape[:3]
    t = t.reshape((Bsz, T // chunk, chunk, H) + t.shape[3:])
    return jnp.moveaxis(t, 2, 3)


def _from_chunks(t):
    N, Bsz, H, C, V = t.shape
    return t.transpose(1, 0, 3, 2, 4).reshape(Bsz, N * C, H, V)


def chunk_gated_delta_rule(q, k, v, g, beta, chunk=DN_CHUNK):
    Bsz, T, H, Kd = q.shape
    Vd = v.shape[-1]
    q = _to_chunks(q * (Kd ** -0.5), chunk)
    k = _to_chunks(k, chunk)
    v = _to_chunks(v, chunk)
    beta = _to_chunks(beta, chunk)
    g = jnp.cumsum(_to_chunks(g, chunk), axis=-1)
    idx = jnp.arange(chunk)
    causal = idx[:, None] >= idx[None, :]
    strict = idx[:, None] > idx[None, :]
    decay = jnp.exp(jnp.where(causal, g[..., :, None] - g[..., None, :], -jnp.inf))
    kb = k * beta[..., None]
    m = jnp.where(strict, jnp.einsum('bnhik,bnhjk->bnhij', kb, k) * decay, 0.0)
    a = m + jnp.eye(chunk, dtype=m.dtype)
    u = lax.linalg.triangular_solve(a, v * beta[..., None], left_side=True, lower=True, unit_diagonal=True)
    w = lax.linalg.triangular_solve(a, kb * jnp.exp(g)[..., None], left_side=True, lower=True, unit_diagonal=True)
    attn = jnp.einsum('bnhik,bnhjk->bnhij', q, k) * decay
    qg = q * jnp.exp(g)[..., None]
    kd = k * jnp.exp(g[..., -1:] - g)[..., None]
    g_last = jnp.exp(g[..., -1])

    def step(S, xs):
        qg_c, kd_c, u_c, w_c, attn_c, gl_c = xs
        v_new = u_c - jnp.einsum('bhck,bhkv->bhcv', w_c, S)
        o = jnp.einsum('bhck,bhkv->bhcv', qg_c, S) + jnp.einsum('bhij,bhjv->bhiv', attn_c, v_new)
        S = S * gl_c[..., None, None] + jnp.einsum('bhck,bhcv->bhkv', kd_c, v_new)
        return S, o

    xs = tuple(jnp.moveaxis(t, 1, 0) for t in (qg, kd, u, w, attn, g_last))
    S0 = jnp.zeros((Bsz, H, Kd, Vd), jnp.float32)
    _, o = lax.scan(step, S0, xs)
    return _from_chunks(o)


def ssd_chunked(xdt, a, Bm, Cm, chunk=SSM_CHUNK):
    Bsz, T, H, P = xdt.shape
    G, N = Bm.shape[2:]
    R = H // G
    Nc = T // chunk
    xdt = xdt.reshape(Bsz, Nc, chunk, G, R, P)
    a = a.reshape(Bsz, Nc, chunk, G, R).transpose(0, 1, 3, 4, 2)
    Bm = Bm.reshape(Bsz, Nc, chunk, G, N)
    Cm = Cm.reshape(Bsz, Nc, chunk, G, N)
    acs = jnp.cumsum(a, axis=-1)
    idx = jnp.arange(chunk)
    causal = idx[:, None] >= idx[None, :]
    lmat = jnp.exp(jnp.where(causal, acs[..., :, None] - acs[..., None, :], -jnp.inf))
    cb = jnp.einsum('bclgn,bcsgn->bcgls', Cm, Bm)
    y_diag = jnp.einsum('bcgls,bcgrls,bcsgrp->bclgrp', cb, lmat, xdt)
    states = jnp.einsum('bclgn,bcgrl,bclgrp->bcgrpn', Bm, jnp.exp(acs[..., -1:] - acs), xdt)
    chunk_decay = jnp.exp(acs[..., -1])

    def step(S, inp):
        st, dec = inp
        return S * dec[..., None, None] + st, S

    S0 = jnp.zeros((Bsz, G, R, P, N), jnp.float32)
    _, s_in = lax.scan(step, S0, (jnp.moveaxis(states, 1, 0), jnp.moveaxis(chunk_decay, 1, 0)))
    s_in = jnp.moveaxis(s_in, 0, 1)
    y_off = jnp.einsum('bclgn,bcgrpn,bcgrl->bclgrp', Cm, s_in, jnp.exp(acs))
    return (y_diag + y_off).reshape(Bsz, T, H, P)


def chunk_gla(q, k, v, gk, chunk=GLA_CHUNK):
    Bsz, T, H, Kd = q.shape
    Vd = v.shape[-1]
    q = _to_chunks(q * (Kd ** -0.5), chunk)
    k = _to_chunks(k, chunk)
    v = _to_chunks(v, chunk)
    G = jnp.cumsum(_to_chunks(gk, chunk), axis=3)
    idx = jnp.arange(chunk)
    causal = (idx[:, None] >= idx[None, :])[:, :, None]
    pair = jnp.exp(jnp.where(causal, G[..., :, None, :] - G[..., None, :, :], -jnp.inf))
    attn = jnp.einsum('bnhik,bnhjk,bnhijk->bnhij', q, k, pair)
    qg = q * jnp.exp(G)
    kd = k * jnp.exp(G[..., -1:, :] - G)
    dec = jnp.exp(G[..., -1, :])

    def step(S, xs):
        qg_c, kd_c, v_c, attn_c, dec_c = xs
        o = jnp.einsum('bhck,bhkv->bhcv', qg_c, S) + jnp.einsum('bhij,bhjv->bhiv', attn_c, v_c)
        S = S * dec_c[..., None] + jnp.einsum('bhck,bhcv->bhkv', kd_c, v_c)
        return S, o

    xs = tuple(jnp.moveaxis(t, 1, 0) for t in (qg, kd, v, attn, dec))
    S0 = jnp.zeros((Bsz, H, Kd, Vd), jnp.float32)
    _, o = lax.scan(step, S0, xs)
    return _from_chunks(o)


def gated_deltanet_branch(q, k, v, beta_logit, a_logit, gate, conv_w, a_log, dt_bias, norm_w):
    dtype = q.dtype
    Bsz, T, _ = q.shape
    qkv = jax.nn.silu(causal_depthwise_conv(jnp.concatenate([q, k, v], axis=-1), conv_w))
    q, k, v = jnp.split(qkv, 3, axis=-1)
    q = l2norm(q.reshape(Bsz, T, DN_HEADS, DN_HEAD_DIM))
    k = l2norm(k.reshape(Bsz, T, DN_HEADS, DN_HEAD_DIM))
    v = v.reshape(Bsz, T, DN_HEADS, DN_HEAD_DIM).astype(jnp.float32)
    beta = jax.nn.sigmoid(beta_logit.astype(jnp.float32))
    g = -jnp.exp(a_log.astype(jnp.float32)) * jax.nn.softplus(a_logit.astype(jnp.float32) + dt_bias.astype(jnp.float32))
    o = chunk_gated_delta_rule(q, k, v, g, beta)
    o = rmsnorm(o, norm_w) * jax.nn.silu(gate.reshape(Bsz, T, DN_HEADS, DN_HEAD_DIM).astype(jnp.float32))
    return o.reshape(Bsz, T, MIX_WIDTH).astype(dtype)


def mamba2_branch(z, xs, Bm, Cm, dt_raw, conv_w, conv_b, dt_bias, a_log, d_skip, norm_w):
    dtype = xs.dtype
    Bsz, T, _ = xs.shape
    xbc = jax.nn.silu(causal_depthwise_conv(jnp.concatenate([xs, Bm, Cm], axis=-1), conv_w, conv_b))
    xs, Bm, Cm = jnp.split(xbc, [MIX_WIDTH, MIX_WIDTH + SSM_GROUPS * SSM_STATE], axis=-1)
    x = xs.reshape(Bsz, T, SSM_HEADS, SSM_HEAD_DIM).astype(jnp.float32)
    Bm = Bm.reshape(Bsz, T, SSM_GROUPS, SSM_STATE).astype(jnp.float32)
    Cm = Cm.reshape(Bsz, T, SSM_GROUPS, SSM_STATE).astype(jnp.float32)
    dt = jax.nn.softplus(dt_raw.astype(jnp.float32) + dt_bias.astype(jnp.float32))
    A = -jnp.exp(a_log.astype(jnp.float32))
    y = ssd_chunked(x * dt[..., None], dt * A, Bm, Cm)
    y = y + d_skip.astype(jnp.float32)[:, None] * x
    y = y.reshape(Bsz, T, MIX_WIDTH) * jax.nn.silu(z.astype(jnp.float32))
    y = rmsnorm(y.reshape(Bsz, T, SSM_GROUPS, MIX_WIDTH // SSM_GROUPS), norm_w.reshape(SSM_GROUPS, -1))
    return y.reshape(Bsz, T, MIX_WIDTH).astype(dtype)


def gla_branch(q, k, v, gate_lr, out_gate, w2, b2, norm_w):
    dtype = q.dtype
    Bsz, T, _ = q.shape
    q = q.reshape(Bsz, T, GLA_HEADS, GLA_K_DIM).astype(jnp.float32)
    k = k.reshape(Bsz, T, GLA_HEADS, GLA_K_DIM).astype(jnp.float32)
    v = v.reshape(Bsz, T, GLA_HEADS, GLA_V_DIM).astype(jnp.float32)
    gk = jax.nn.log_sigmoid(jnp.einsum('btr,rk->btk', gate_lr.astype(jnp.float32), w2.astype(jnp.float32))
                            + b2.astype(jnp.float32)) / GLA_GATE_TEMP
    gk = gk.reshape(Bsz, T, GLA_HEADS, GLA_K_DIM)
    o = chunk_gla(q, k, v, gk)
    o = rmsnorm(o, norm_w) * jax.nn.silu(out_gate.reshape(Bsz, T, GLA_HEADS, GLA_V_DIM).astype(jnp.float32))
    return o.reshape(Bsz, T, MIX_WIDTH).astype(dtype)


def setup_inputs(seed: int = 0) -> dict:
    key = jax.random.key(seed)
    ks = iter(jax.random.split(key, 40))

    def nrm(shape, scale):
        return jax.random.normal(next(ks), shape, jnp.float32) * scale

    def gain(shape):
        return 1.0 + nrm(shape, 0.02)

    def log_a(n):
        return jnp.log(jax.random.uniform(next(ks), (DEPTH, n), jnp.float32, 1.0, 16.0))

    def dt_bias(n):
        dt = jnp.exp(jax.random.uniform(next(ks), (DEPTH, n), jnp.float32, np.log(1e-3), np.log(1e-1)))
        return jnp.log(jnp.expm1(dt))

    return {
        "x": nrm((BATCH, SEQ, D_MODEL), 1.0),
        "p": nrm((DEPTH, BATCH, SEQ, PLE_DIM), 1.0),
        "pre_mix_norm": gain((DEPTH, D_MODEL)),
        "w_in": nrm((DEPTH, D_MODEL, IN_TOTAL), D_MODEL ** -0.5),
        "dn_conv_w": nrm((DEPTH, CONV_WIDTH, 3 * MIX_WIDTH), CONV_WIDTH ** -0.5),
        "dn_a_log": log_a(DN_HEADS),
        "dn_dt_bias": dt_bias(DN_HEADS),
        "dn_norm": gain((DEPTH, DN_HEAD_DIM)),
        "ssm_conv_w": nrm((DEPTH, CONV_WIDTH, MIX_WIDTH + 2 * SSM_GROUPS * SSM_STATE), CONV_WIDTH ** -0.5),
        "ssm_conv_b": nrm((DEPTH, MIX_WIDTH + 2 * SSM_GROUPS * SSM_STATE), 0.02),
        "ssm_dt_bias": dt_bias(SSM_HEADS),
        "ssm_a_log": log_a(SSM_HEADS),
        "ssm_d": gain((DEPTH, SSM_HEADS)),
        "ssm_norm": gain((DEPTH, MIX_WIDTH)),
        "gla_gate_w2": nrm((DEPTH, GLA_GATE_RANK, GLA_K_WIDTH), GLA_GATE_RANK ** -0.5),
        "gla_gate_b": nrm((DEPTH, GLA_K_WIDTH), 0.1),
        "gla_norm": gain((DEPTH, GLA_V_DIM)),
        "w_branch": nrm((DEPTH, N_BRANCH, MIX_WIDTH, D_MODEL), MIX_WIDTH ** -0.5),
        "w_out": nrm((DEPTH, D_MODEL, D_MODEL), D_MODEL ** -0.5),
        "post_mix_norm": gain((DEPTH, D_MODEL)),
        "pre_mlp_norm": gain((DEPTH, D_MODEL)),
        "w_up": nrm((DEPTH, D_MODEL, D_FF), D_MODEL ** -0.5),
        "w_down": nrm((DEPTH, D_FF, D_MODEL), D_FF ** -0.5),
        "post_mlp_norm": gain((DEPTH, D_MODEL)),
        "ple_pre_norm": gain((DEPTH, D_MODEL)),
        "w_ple_gate": nrm((DEPTH, D_MODEL, D_MODEL), D_MODEL ** -0.5),
        "w_ple_proj": nrm((DEPTH, PLE_DIM, D_MODEL), PLE_DIM ** -0.5),
        "ple_post_norm": gain((DEPTH, D_MODEL)),
    }


def reference(x, p, pre_mix_norm, w_in, dn_conv_w, dn_a_log, dn_dt_bias, dn_norm,
              ssm_conv_w, ssm_conv_b, ssm_dt_bias, ssm_a_log, ssm_d, ssm_norm,
              gla_gate_w2, gla_gate_b, gla_norm, w_branch, w_out, post_mix_norm,
              pre_mlp_norm, w_up, w_down, post_mlp_norm,
              ple_pre_norm, w_ple_gate, w_ple_proj, ple_post_norm):
    Bsz, T, D = x.shape
    split_idx = np.cumsum(IN_SPLITS)[:-1].tolist()
    for i in range(DEPTH):
        h = rmsnorm(x, pre_mix_norm[i])
        (dn_q, dn_k, dn_v, dn_b, dn_a, dn_g,
         s_z, s_x, s_B, s_C, s_dt,
         g_q, g_k, g_v, g_lr, g_o, br_gate) = jnp.split(h @ w_in[i], split_idx, axis=-1)
        y_dn = gated_deltanet_branch(dn_q, dn_k, dn_v, dn_b, dn_a, dn_g,
                                     dn_conv_w[i], dn_a_log[i], dn_dt_bias[i], dn_norm[i])
        y_ssm = mamba2_branch(s_z, s_x, s_B, s_C, s_dt, ssm_conv_w[i], ssm_conv_b[i],
                              ssm_dt_bias[i], ssm_a_log[i], ssm_d[i], ssm_norm[i])
        y_gla = gla_branch(g_q, g_k, g_v, g_lr, g_o, gla_gate_w2[i], gla_gate_b[i], gla_norm[i])
        branches = jnp.stack([y_dn, y_ssm, y_gla], axis=2)
        up = jnp.einsum('btnm,nmd->btnd', branches, w_branch[i])
        gates = jax.nn.sigmoid(br_gate.reshape(Bsz, T, N_BRANCH, D))
        mixed = jnp.sum(gates * up, axis=2) @ w_out[i]
        x = x + rmsnorm(mixed, post_mix_norm[i])
        h = rmsnorm(x, pre_mlp_norm[i])
        m = jnp.square(jax.nn.relu(h @ w_up[i])) @ w_down[i]
        x = x + rmsnorm(m, post_mlp_norm[i])
        ple_gate = jax.nn.sigmoid(rmsnorm(x, ple_pre_norm[i]) @ w_ple_gate[i])
        e = (p[i] @ w_ple_proj[i]) * ple_gate
        x = x + rmsnorm(e, ple_post_norm[i])
    return x
```

```python
import numpy as np
from contextlib import ExitStack
import concourse.bass as bass
import concourse.mybir as mybir
from concourse.bass_utils import run_bass_kernel_spmd

F32 = mybir.dt.float32
BF16 = mybir.dt.bfloat16
ALU = mybir.AluOpType
AF = mybir.ActivationFunctionType

D = 2048
KC = 16
EPS = 1e-6
NEG = -30000.0
PE_ = "dve"


class Trk:
    __slots__ = ("w", "r", "dsem", "dcnt", "name", "excl")

    def __init__(self, name):
        self.excl = False
        self.w = None
        self.r = {}
        self.dsem = None
        self.dcnt = 0
        self.name = name


class V:
    __slots__ = ("ap", "trk")

    def __init__(self, ap, trk):
        self.ap = ap
        self.trk = trk

    def __getitem__(self, idx):
        return V(self.ap[idx], self.trk)


class TT:
    def __init__(self, ap, name):
        self.ap = ap
        self.trk = Trk(name)

    def __getitem__(self, idx):
        return V(self.ap[idx], self.trk)

    def view(self, idx):
        t = TT(self.ap[idx], self.trk.name)
        t.trk = self.trk
        return t

    def sub(self, idx, name=None):
        return TT(self.ap[idx], name or self.trk.name + "_sub")


class FW:
    def __init__(self, nc, es):
        self.nc = nc
        self.es = es
        self.eng = {"pe": nc.tensor, "dve": nc.vector, "act": nc.scalar, "pool": nc.gpsimd, "sp": nc.sync}
        self.sem = {}
        self.cnt = {}
        self.waited = {}
        for e in self.eng:
            self.sem[e] = es.enter_context(nc.semaphore("sem_" + e))
            self.cnt[e] = 0
            self.waited[e] = {}
        self.semowner = {id(self.sem[e]): e for e in self.eng}
        self.nsem = len(self.eng)
        self.out_dmas = []
        self.uid = 0

    def sb(self, name, shape, dtype=F32):
        n = 1
        for d in shape[1:]:
            n *= d
        self.sb_bytes = getattr(self, "sb_bytes", 0) + n * (2 if dtype == BF16 else 4)
        return TT(self.es.enter_context(self.nc.sbuf_tensor("s_" + name, list(shape), dtype)), name)

    def ps(self, name, shape=(128, 512), dtype=F32):
        t = TT(self.es.enter_context(self.nc.psum_tensor("p_" + name, list(shape), dtype)), name)
        t.trk.excl = True
        return t

    def _wait(self, e, dep):
        sem, val = dep
        k = id(sem)
        if self.semowner.get(k) == e and e == "pe":
            return
        if self.waited[e].get(k, 0) >= val:
            return
        self.eng[e].wait_ge(sem, val)
        self.waited[e][k] = val

    def _sync(self, e, reads, writes):
        for t in reads:
            if t.w is not None:
                self._wait(e, t.w)
        for t in writes:
            if t.w is not None:
                self._wait(e, t.w)
            for k, dep in t.r.items():
                self._wait(e, dep)

    def op(self, e, name, **kw):
        reads, writes = [], []
        args = {}
        for k, v in kw.items():
            if isinstance(v, V):
                (writes if (k in ("out", "accum_out") or v.trk.excl) else reads).append(v.trk)
                args[k] = v.ap
            else:
                args[k] = v
        self._sync(e, reads, writes)
        inst = getattr(self.eng[e], name)(**args)
        self.cnt[e] += 1
        inst.then_inc(self.sem[e], 1)
        me = (self.sem[e], self.cnt[e])
        for t in reads:
            t.r[id(self.sem[e])] = me
        for t in writes:
            t.w = me
            t.r = {}
        return inst

    def dma(self, e, out, in_, **kw):
        if isinstance(out, V):
            t = out.trk
            self._sync(e, [], [t])
            if t.dsem is None:
                t.dsem = self.es.enter_context(self.nc.semaphore("dsem%d" % self.nsem))
                self.nsem += 1
            self.eng[e].dma_start(out=out.ap, in_=in_, **kw).then_inc(t.dsem, 16)
            t.dcnt += 16
            t.w = (t.dsem, t.dcnt)
            t.r = {}
        else:
            t = in_.trk
            self._sync(e, [t], [])
            if t.dsem is None:
                t.dsem = self.es.enter_context(self.nc.semaphore("dsem%d" % self.nsem))
                self.nsem += 1
            self.eng[e].dma_start(out=out, in_=in_.ap, **kw).then_inc(t.dsem, 16)
            t.dcnt += 16
            t.r[id(t.dsem)] = (t.dsem, t.dcnt)
            self.out_dmas.append((t.dsem, t.dcnt))

    def alias_barrier(self, new, old):
        deps = {}
        for t in old:
            for dep in ([t.trk.w] if t.trk.w is not None else []) + list(t.trk.r.values()):
                k = id(dep[0])
                if k not in deps or deps[k][1] < dep[1]:
                    deps[k] = dep
        for t in new:
            for k, dep in deps.items():
                if k not in t.trk.r or t.trk.r[k][1] < dep[1]:
                    t.trk.r[k] = dep

    def finish(self):
        last = {}
        for sem, val in self.out_dmas:
            last[id(sem)] = (sem, val)
        for sem, val in last.values():
            self._wait("sp", (sem, val))

    def mm(self, out, lhsT, rhs, start=True, stop=True):
        return self.op("pe", "matmul", out=out, lhsT=lhsT, rhs=rhs, start=start, stop=stop)

    def tr(self, out, in_, ident):
        return self.op("pe", "transpose", out=out, in_=in_, identity=ident)

    def act(self, out, in_, func, bias=None, scale=None, e="act"):
        kw = {}
        if bias is not None:
            kw["bias"] = bias
        if scale is not None:
            kw["scale"] = scale
        return self.op(e, "activation", out=out, in_=in_, func=func, **kw)

    def ts(self, out, in0, s1, op0, s2=None, op1=None, e="dve"):
        if op1 is None:
            return self.op(e, "tensor_scalar", out=out, in0=in0, scalar1=s1, scalar2=None, op0=op0)
        return self.op(e, "tensor_scalar", out=out, in0=in0, scalar1=s1, scalar2=s2, op0=op0, op1=op1)

    def stt(self, out, in0, scalar, in1, op0, op1):
        return self.op("dve", "scalar_tensor_tensor", out=out, in0=in0, scalar=scalar, in1=in1, op0=op0, op1=op1)

    def tt(self, out, in0, in1, op, e="dve"):
        return self.op(e, "tensor_tensor", out=out, in0=in0, in1=in1, op=op)

    def cp(self, out, in_, e="act"):
        if e == "act":
            return self.op("act", "copy", out=out, in_=in_)
        return self.op(e, "tensor_copy", out=out, in_=in_)


C_ID, C_ONES, C_TRI, C_BLK, C_NMT, C_PMS, C_M01, C_RST = range(8)
NCONST = 8


def make_consts():
    p = np.arange(128)[:, None]
    f = np.arange(128)[None, :]
    same = (p // 64) == (f // 64)
    c = np.zeros((128, NCONST, 128), np.float32)
    c[:, C_ID] = (p == f)
    c[:, C_ONES] = 1.0
    c[:, C_TRI] = (same & (p <= f))
    c[:, C_BLK] = same
    c[:, C_NMT] = np.where(same & (f >= p), 0.0, NEG)
    c[:, C_PMS] = np.where(same & (p > f), 0.0, -NEG)
    c[:, C_M01] = (same & (f >= p))
    c[:, C_RST] = ((f % 64) != 0)
    return c


NFM = 8 * 128 + 16
NTM = 132
S_DN_ALOG, S_DN_DTB, S_S_DTB0, S_S_DTB1, S_S_ALOG0, S_S_ALOG1, S_S_D0, S_S_D1 = range(8)


class _StopStage(Exception):
    pass


def build_mixer(T, ST=256, parts=("dn", "ssm", "gla"), dn_stage=99):
    def stage(k):
        if dn_stage == k:
            raise _StopStage()

    nc = bass.Bass("TRN2", target_bir_lowering=False)
    NST = T // ST
    NSUB = ST // 128

    def din(name, shape):
        return nc.dram_tensor(name, list(shape), F32, kind="ExternalInput").ap()

    xT_d = din("xT", [128, KC, T])
    gain_d = din("gain", [128, KC])
    wfm_d = din("wfm", [128, KC, NFM])
    wtm_d = din("wtm", [128, KC, NTM])
    cw_d = din("cw", [128, 6, 4])
    cb_d = din("cb", [128, 3])
    scal_d = din("scal", [128, 8])
    w2_d = din("w2", [16, 128])
    gb_d = din("gb", [128, 1])
    const_d = din("consts", [128, NCONST, 128])
    odn_d = nc.dram_tensor("o_dn", [T, 128], F32, kind="ExternalOutput").ap()
    ossm_d = nc.dram_tensor("o_ssm", [T, 128], F32, kind="ExternalOutput").ap()
    ogla_d = nc.dram_tensor("o_gla", [T, 128], F32, kind="ExternalOutput").ap()

    with ExitStack() as es:
        fw = FW(nc, es)
        sb, mm, tr, act, ts, stt, tt, cp = fw.sb, fw.mm, fw.tr, fw.act, fw.ts, fw.stt, fw.tt, fw.cp

        wfm = sb("wfm", [128, KC, NFM], BF16)
        wtm = sb("wtm", [128, KC, NTM], BF16)
        gain = sb("gain", [128, KC])
        cw = sb("cw", [128, 6, 4])
        cb = sb("cb", [128, 3])
        scal = sb("scal", [128, 8])
        w2 = sb("w2", [16, 128])
        gb = sb("gb", [128, 1])
        consts = sb("consts", [128, NCONST, 128])
        fw.dma("sp", consts[:], const_d)
        fw.dma("sp", gain[:], gain_d)
        fw.dma("sp", cw[:], cw_d)
        fw.dma("sp", cb[:], cb_d)
        fw.dma("sp", scal[:], scal_d)
        fw.dma("sp", w2[:], w2_d)
        fw.dma("sp", gb[:], gb_d)
        wparts = []
        for g in range(4):
            wp = wfm.sub((slice(None), slice(g * 4, g * 4 + 4), slice(None)), "wfm%d" % g)
            fw.dma("pool", wp[:], wfm_d[:, g * 4:g * 4 + 4, :])
            wparts.append(wp)
        fw.dma("pool", wtm[:], wtm_d)

        def wf(kc, c0, c1):
            return wparts[kc // 4][:, kc % 4, c0:c1]

        def cst(i):
            return consts[:, i, :]

        ident, ones, tri, blk = cst(C_ID), cst(C_ONES), cst(C_TRI), cst(C_BLK)
        nmt, pms, m01, rst = cst(C_NMT), cst(C_PMS), cst(C_M01), cst(C_RST)

        ones_bf = sb("ones_bf", [128, 128], BF16)
        cp(ones_bf[:], ones, e="dve")
        nega = sb("nega", [128, 4])
        act(nega[:, 0:1], scal[:, S_DN_ALOG:S_DN_ALOG + 1], AF.Exp)
        act(nega[:, 1:3], scal[:, S_S_ALOG0:S_S_ALOG1 + 1], AF.Exp)
        ts(nega[:, 0:3], nega[:, 0:3], -1.0, ALU.mult)
        ngb = sb("ngb", [128, 1])
        ts(ngb[:], gb[:], -1.0, ALU.mult)

        xt = [sb("xt%d" % i, [128, KC, ST]) for i in range(1)]
        sqb = [sb("sqb%d" % i, [128, ST], BF16) for i in range(2)]
        hT = sb("hT", [128, KC, ST], BF16)
        lnt = sb("lnt", [128, ST])
        rstd = sb("rstd", [128, ST])
        raw = [sb("raw%d" % b, [128, ST + 3]) for b in range(6)]
        cva = [sb("cva%d" % b, [128, ST]) for b in range(6)]
        cvo = [sb("cvo%d" % b, [128, ST]) for b in range(6)]
        gq = sb("gq", [128, ST])
        gk_ = sb("gk", [128, ST])
        lrT = sb("lrT", [16, ST])
        sqf = sb("sqf", [128, ST])
        qn = sb("qn", [128, ST])
        kn = sb("kn", [128, ST])
        for b in range(6):
            ts(raw[b][:, 0:3], consts[:, C_ONES, 0:3], 0.0, ALU.mult)

        pacc = [fw.ps("pacc%d" % i) for i in range(2)]
        pn = fw.ps("pn")
        ptm = fw.ps("ptm")
        pbanks = [fw.ps("pb%d" % i) for i in range(4)]
        slots = []
        for bnk in pbanks:
            slots.append(bnk.view((slice(None), slice(0, 128))))
        slot_i = [0]

        def pslot():
            s = slots[slot_i[0] % len(slots)]
            slot_i[0] += 1
            return s

        tmp_i = {}

        def tmp(name, shape=(128, 128), n=2, dtype=F32):
            key = name
            if key not in tmp_i:
                tmp_i[key] = [0, [sb("%s_%d" % (name, i), shape, dtype) for i in range(n)]]
            ent = tmp_i[key]
            t = ent[1][ent[0] % n]
            ent[0] += 1
            return t

        S_dn = sb("S_dn", [128, 128])
        S_ssm = sb("S_ssm", [128, 128])
        S_gla = sb("S_gla", [128, 128])
        for S in (S_dn, S_ssm, S_gla):
            ts(S[:], ones, 0.0, ALU.mult)

        KS = 128.0 ** -0.5

        for st in range(NST):
            t0 = st * ST
            x = xt[st % len(xt)]
            fw.dma("sp", x[:], xT_d[:, :, t0:t0 + ST])
            for kc in range(KC):
                sq = sqb[kc % 2]
                act(sq[:], x[:, kc, :], AF.Square)
                mm(pn[:, 0:ST], ones_bf[:], sq[:], start=(kc == 0), stop=(kc == KC - 1))
            act(lnt[:], pn[:, 0:ST], AF.Ln, bias=EPS, scale=1.0 / D)
            act(rstd[:], lnt[:], AF.Exp, scale=-0.5)
            for kc in range(KC):
                stt(hT[:, kc, :], x[:, kc, :], gain[:, kc:kc + 1], rstd[:], ALU.mult, ALU.mult)
            for b in range(9):
                M = 128 if b < 8 else 16
                pa = pacc[b % 2]
                for kc in range(KC):
                    mm(pa[0:M, 0:ST], wf(kc, b * 128, b * 128 + M), hT[:, kc, :], start=(kc == 0), stop=(kc == KC - 1))
                if b < 6:
                    cp(raw[b][:, 3:3 + ST], pa[:, 0:ST])
                elif b == 6:
                    cp(gq[:], pa[:, 0:ST])
                elif b == 7:
                    cp(gk_[:], pa[:, 0:ST])
                else:
                    cp(lrT[:], pa[0:16, 0:ST])
            for b in range(6):
                if b >= 3:
                    ts(cva[b][:], raw[b][:, 0:ST], cw[:, b, 0:1], ALU.mult, cb[:, b - 3:b - 2], ALU.add)
                else:
                    ts(cva[b][:], raw[b][:, 0:ST], cw[:, b, 0:1], ALU.mult)
                for k in range(1, 4):
                    stt(cva[b][:], raw[b][:, k:k + ST], cw[:, b, k:k + 1], cva[b][:], ALU.mult, ALU.add)
                act(cvo[b][:], cva[b][:], AF.Silu)
                cp(raw[b][:, 0:3], raw[b][:, ST:ST + 3], e="dve")
            for src, dst in ((cvo[0], qn), (cvo[1], kn)):
                act(sqf[:], src[:], AF.Square)
                mm(pn[:, 0:ST], ones, sqf[:])
                act(lnt[:], pn[:, 0:ST], AF.Ln, bias=EPS)
                act(lnt[:], lnt[:], AF.Exp, scale=-0.5)
                tt(dst[:], src[:], lnt[:], ALU.mult)

            for s in range(NSUB):
                c0 = s * 128
                cs = slice(c0, c0 + 128)
                tg = t0 + c0
                for kc in range(KC):
                    mm(ptm[:, 0:NTM], hT[:, kc, cs], wtm[:, kc, :], start=(kc == 0), stop=(kc == KC - 1))
                tm = tmp("tm", (128, NTM))
                cp(tm[:], ptm[:, 0:NTM])

                if "dn" in parts:
                    try:
                        sm = tmp("dn_sm", (128, 16))
                        act(sm[:, 0:1], tm[:, 128:129], AF.Sigmoid)
                        act(sm[:, 1:2], tm[:, 129:130], AF.Exp, bias=scal[:, S_DN_DTB:S_DN_DTB + 1])
                        act(sm[:, 2:3], sm[:, 1:2], AF.Ln, bias=1.0)
                        ts(sm[:, 3:4], sm[:, 2:3], nega[:, 0:1], ALU.mult)
                        ts(sm[:, 4:5], sm[:, 0:1], -1.0, ALU.mult)
                        Gbc = tmp("Gbc")
                        ts(Gbc[:], ones, sm[:, 3:4], ALU.mult)
                        p_gc = pslot()
                        mm(p_gc[:], Gbc[:], tri)
                        p_c = pslot()
                        mm(p_c[:, 0:1], tri, sm[:, 3:4])
                        mm(p_c[:, 1:2], blk, sm[:, 3:4])
                        cp(sm[:, 5:7], p_c[:, 0:2], e="dve")
                        ET = tmp("ET")
                        stt(ET[:], p_gc[:], sm[:, 5:6], nmt, ALU.subtract, ALU.add)
                        decT = tmp("decT")
                        act(decT[:], ET[:], AF.Exp)
                        ES = tmp("ES")
                        stt(ES[:], p_gc[:], sm[:, 5:6], pms, ALU.subtract, ALU.add)
                        decS = tmp("decS")
                        act(decS[:], ES[:], AF.Exp, scale=-1.0)
                        egbc = tmp("egbc")
                        act(egbc[:], p_gc[:], AF.Exp)
                        act(sm[:, 7:8], sm[:, 5:6], AF.Exp)
                        tt(sm[:, 10:11], sm[:, 6:7], sm[:, 5:6], ALU.subtract)
                        act(sm[:, 8:9], sm[:, 10:11], AF.Exp)
                        tt(sm[:, 9:10], sm[:, 0:1], sm[:, 7:8], ALU.mult)
                        stage(1)
                        p_kk = pslot()
                        mm(p_kk[:], kn[:, cs], kn[:, cs])
                        p_qk = pslot()
                        mm(p_qk[:], kn[:, cs], qn[:, cs])
                        Nm = tmp("Nm", n=3)
                        stt(Nm[:], p_kk[:], sm[:, 4:5], decS[:], ALU.mult, ALU.mult)
                        attnT = tmp("attnT")
                        stt(attnT[:], p_qk[:], KS, decT[:], ALU.mult, ALU.mult)
                        stage(2)
                        p_b = pslot()
                        tr(p_b[:], Nm[:], ident)
                        Bm = tmp("Bm", n=3)
                        cp(Bm[:], p_b[:])
                        R = tmp("R", n=3)
                        tt(R[:], p_b[:], ident, ALU.add)
                        stage(3)
                        curN, curB = Nm, Bm
                        for step in range(5):
                            p_n2 = pslot()
                            mm(p_n2[:], curB[:], curN[:])
                            N2 = tmp("Nm", n=3)
                            cp(N2[:], p_n2[:])
                            if step < 4:
                                p_b2 = pslot()
                                mm(p_b2[:], curN[:], curB[:])
                                B2 = tmp("Bm", n=3)
                                cp(B2[:], p_b2[:], e="dve")
                            p_r = pslot()
                            mm(p_r[:], N2[:], R[:])
                            R2 = tmp("R", n=3)
                            tt(R2[:], p_r[:], R[:], ALU.add)
                            R = R2
                            curN = N2
                            if step < 4:
                                curB = B2
                        stage(4)
                        p_kt = pslot()
                        tr(p_kt[:], kn[:, cs], ident)
                        kbg = tmp("kbg")
                        ts(kbg[:], p_kt[:], sm[:, 9:10], ALU.mult)
                        kd = tmp("kd")
                        ts(kd[:], p_kt[:], sm[:, 8:9], ALU.mult)
                        p_vt = pslot()
                        tr(p_vt[:], cvo[2][:, cs], ident)
                        bv = tmp("bv")
                        ts(bv[:], p_vt[:], sm[:, 0:1], ALU.mult)
                        stage(5)
                        p_w = pslot()
                        mm(p_w[:], kbg[:], R[:])
                        wT = tmp("wT")
                        cp(wT[:], p_w[:])
                        p_u = pslot()
                        mm(p_u[:], R[:], bv[:])
                        u = tmp("u")
                        cp(u[:], p_u[:])
                        stage(6)
                        qgT = tmp("qgT")
                        stt(qgT[:], qn[:, cs], KS, egbc[:], ALU.mult, ALU.mult)
                        o_dn = tmp("o_dn")
                        vnew = tmp("vnew")
                        for c in range(2):
                            r = slice(64 * c, 64 * c + 64)
                            p1 = pslot()
                            mm(p1[:], wT[:], S_dn[:])
                            tt(vnew[r, :], u[r, :], p1[r, :], ALU.subtract)
                            po = pslot()
                            mm(po[:], qgT[:], S_dn[:], start=True, stop=False)
                            mm(po[:], attnT[r, :], vnew[r, :], start=False, stop=True)
                            cp(o_dn[r, :], po[r, :])
                            p_s = pslot()
                            mm(p_s[:], kd[r, :], vnew[r, :])
                            gl = egbc[:, 64 * c + 63:64 * c + 64]
                            stt(S_dn[:], S_dn[:], gl, p_s[:], ALU.mult, ALU.add)
                        stage(8)
                        fw.dma("sp", odn_d[tg:tg + 128, :], o_dn[:])

                    except _StopStage:
                        pass
                if "ssm" in parts:
                    ss = tmp("ss_sm", (128, 16))
                    act(ss[:, 0:1], tm[:, 130:131], AF.Exp, bias=scal[:, S_S_DTB0:S_S_DTB0 + 1])
                    act(ss[:, 1:2], tm[:, 131:132], AF.Exp, bias=scal[:, S_S_DTB1:S_S_DTB1 + 1])
                    act(ss[:, 2:4], ss[:, 0:2], AF.Ln, bias=1.0)
                    tt(ss[:, 4:6], ss[:, 2:4], nega[:, 1:3], ALU.mult)
                    p_c = pslot()
                    mm(p_c[:, 0:2], tri, ss[:, 4:6])
                    mm(p_c[:, 2:4], blk, ss[:, 4:6])
                    cp(ss[:, 6:10], p_c[:, 0:4], e="dve")
                    tt(ss[:, 12:14], ss[:, 8:10], ss[:, 6:8], ALU.subtract)
                    act(ss[:, 10:12], ss[:, 12:14], AF.Exp)
                    xT_s, BT_s, CT_s = cvo[3][:, cs], cvo[4][:, cs], cvo[5][:, cs]
                    p_cb = pslot()
                    mm(p_cb[:], BT_s, CT_s)
                    WT, CgT, eab = [], [], []
                    for h in range(2):
                        Abc = tmp("Abc")
                        ts(Abc[:], ones, ss[:, 4 + h:5 + h], ALU.mult)
                        p_a = pslot()
                        mm(p_a[:], Abc[:], tri)
                        ETs = tmp("ETs")
                        stt(ETs[:], p_a[:], ss[:, 6 + h:7 + h], nmt, ALU.subtract, ALU.add)
                        LT = tmp("LT")
                        act(LT[:], ETs[:], AF.Exp)
                        ea = tmp("eab", n=4)
                        act(ea[:], p_a[:], AF.Exp)
                        w_ = tmp("WT", n=4)
                        tt(w_[:], p_cb[:], LT[:], ALU.mult)
                        cg = tmp("CgT", n=4)
                        tt(cg[:], CT_s, ea[:], ALU.mult, e=PE_)
                        WT.append(w_)
                        CgT.append(cg)
                        eab.append(ea)
                    p_xt = pslot()
                    tr(p_xt[:], xT_s, ident)
                    xs = tmp("xs")
                    cp(xs[:], p_xt[:])
                    xdt = tmp("xdt")
                    xdd = tmp("xdd")
                    for h in range(2):
                        hc = slice(64 * h, 64 * h + 64)
                        ts(xdt[:, hc], p_xt[:, hc], ss[:, 2 + h:3 + h], ALU.mult)
                        ts(xdd[:, hc], xdt[:, hc], ss[:, 10 + h:11 + h], ALU.mult, e=PE_)
                    p_bt = pslot()
                    tr(p_bt[:], BT_s, ident)
                    Btok = tmp("Btok")
                    cp(Btok[:], p_bt[:])
                    y_ssm = tmp("y_ssm")
                    for c in range(2):
                        r = slice(64 * c, 64 * c + 64)
                        po = pslot()
                        for h in range(2):
                            hc = slice(64 * h, 64 * h + 64)
                            mm(po[:, hc], CgT[h][:], S_ssm[:, hc], start=True, stop=False)
                            mm(po[:, hc], WT[h][r, :], xdt[r, hc], start=False, stop=True)
                        for h in range(2):
                            hc = slice(64 * h, 64 * h + 64)
                            stt(y_ssm[r, hc], xs[r, hc], scal[r, S_S_D0 + h:S_S_D0 + h + 1], po[r, hc], ALU.mult, ALU.add)
                        p_s = pslot()
                        mm(p_s[:], Btok[r, :], xdd[r, :])
                        for h in range(2):
                            hc = slice(64 * h, 64 * h + 64)
                            stt(S_ssm[:, hc], S_ssm[:, hc], eab[h][:, 64 * c + 63:64 * c + 64], p_s[:, hc], ALU.mult, ALU.add)
                    fw.dma("sp", ossm_d[tg:tg + 128, :], y_ssm[:])

                if "gla" in parts:
                    p_g = pslot()
                    mm(p_g[:], w2[:], lrT[:, cs])
                    eg = tmp("g_e")
                    act(eg[:], p_g[:], AF.Exp, bias=ngb[:], scale=-1.0)
                    lT = tmp("g_l")
                    act(lT[:], eg[:], AF.Ln, bias=1.0)
                    LcT = tmp("g_Lc")
                    fw.op("dve", "tensor_tensor_scan", out=LcT[:], data0=rst, data1=lT[:], initial=0.0,
                          op0=ALU.mult, op1=ALU.add)
                    gs = tmp("g_sm", (128, 4))
                    ts(gs[:, 0:1], LcT[:, 63:64], -1.0 / 16.0, ALU.mult)
                    ts(gs[:, 1:2], LcT[:, 127:128], -1.0 / 16.0, ALU.mult)
                    egT = tmp("g_eg")
                    act(egT[:], LcT[:], AF.Exp, scale=-1.0 / 16.0)
                    engT = tmp("g_eng")
                    act(engT[:], LcT[:], AF.Exp, scale=1.0 / 16.0)
                    edT = tmp("g_ed")
                    for c in range(2):
                        act(edT[:, 64 * c:64 * c + 64], LcT[:, 64 * c:64 * c + 64], AF.Exp, bias=gs[:, c:c + 1], scale=1.0 / 16.0)
                    qpT = tmp("g_qp")
                    stt(qpT[:], gq[:, cs], KS, egT[:], ALU.mult, ALU.mult)
                    kppT = tmp("g_kpp")
                    tt(kppT[:], gk_[:, cs], engT[:], ALU.mult, e=PE_)
                    kdT = tmp("g_kdT")
                    tt(kdT[:], gk_[:, cs], edT[:], ALU.mult, e=PE_)
                    p_kd = pslot()
                    tr(p_kd[:], kdT[:], ident)
                    kdg = tmp("g_kd")
                    cp(kdg[:], p_kd[:])
                    p_at = pslot()
                    mm(p_at[:], kppT[:], qpT[:])
                    atg = tmp("g_at")
                    tt(atg[:], p_at[:], m01, ALU.mult)
                    o_gla = tmp("o_gla")
                    vt = tm[:, 0:128]
                    for c in range(2):
                        r = slice(64 * c, 64 * c + 64)
                        po = pslot()
                        mm(po[:], qpT[:], S_gla[:], start=True, stop=False)
                        mm(po[:], atg[r, :], vt[r, :], start=False, stop=True)
                        cp(o_gla[r, :], po[r, :])
                        p_s = pslot()
                        mm(p_s[:], kdg[r, :], vt[r, :])
                        stt(S_gla[:], S_gla[:], egT[:, 64 * c + 63:64 * c + 64], p_s[:], ALU.mult, ALU.add)
                    fw.dma("sp", ogla_d[tg:tg + 128, :], o_gla[:])
        fw.finish()
        print("mixer sbuf bytes/partition", fw.sb_bytes, "instr", fw.cnt)
    return nc


O_DNQ, O_DNK, O_DNV, O_DNB, O_DNA, O_DNG = 0, 1024, 2048, 3072, 3080, 3088
O_SZ, O_SX, O_SB, O_SC, O_SDT = 4112, 5136, 6160, 6416, 6672
O_GQ, O_GK, O_GV, O_GLR, O_GO, O_BR = 6688, 7200, 7712, 8736, 8752, 9776
IN_TOTAL = 15920


def fm_layout(a):
    R, C = a.shape
    return np.ascontiguousarray(a.reshape(R // 128, 128, C).transpose(1, 0, 2))


def mixer_core_inputs(c, L, xT_l, P, consts):
    w_in = P["w_in"][L]
    g = c // 4
    hg, half = c // 2, c % 2
    fm_cols = np.concatenate([
        np.arange(O_DNQ + 128 * c, O_DNQ + 128 * c + 128),
        np.arange(O_DNK + 128 * c, O_DNK + 128 * c + 128),
        np.arange(O_DNV + 128 * c, O_DNV + 128 * c + 128),
        np.arange(O_SX + 128 * c, O_SX + 128 * c + 128),
        np.arange(O_SB + 128 * g, O_SB + 128 * g + 128),
        np.arange(O_SC + 128 * g, O_SC + 128 * g + 128),
        np.arange(O_GQ + 128 * hg, O_GQ + 128 * hg + 128),
        np.arange(O_GK + 128 * hg, O_GK + 128 * hg + 128),
        np.arange(O_GLR, O_GLR + 16)])
    tm_cols = np.concatenate([
        np.arange(O_GV + 256 * hg + 128 * half, O_GV + 256 * hg + 128 * half + 128),
        [O_DNB + c, O_DNA + c, O_SDT + 2 * c, O_SDT + 2 * c + 1]]).astype(np.int64)
    dcw = P["dn_conv_w"][L]
    scw = P["ssm_conv_w"][L]
    scb = P["ssm_conv_b"][L]
    chans = [(dcw, 128 * c), (dcw, 1024 + 128 * c), (dcw, 2048 + 128 * c),
             (scw, 128 * c), (scw, 1024 + 128 * g), (scw, 1280 + 128 * g)]
    cw = np.stack([w[:, o:o + 128].T for (w, o) in chans], axis=1)
    cb = np.stack([scb[o:o + 128] for o in (128 * c, 1024 + 128 * g, 1280 + 128 * g)], axis=1)
    sc = np.array([P["dn_a_log"][L][c], P["dn_dt_bias"][L][c],
                   P["ssm_dt_bias"][L][2 * c], P["ssm_dt_bias"][L][2 * c + 1],
                   P["ssm_a_log"][L][2 * c], P["ssm_a_log"][L][2 * c + 1],
                   P["ssm_d"][L][2 * c], P["ssm_d"][L][2 * c + 1]], np.float32)
    return {
        "xT": xT_l,
        "gain": np.ascontiguousarray(P["pre_mix_norm"][L].reshape(KC, 128).T),
        "wfm": fm_layout(w_in[:, fm_cols]),
        "wtm": fm_layout(w_in[:, tm_cols]),
        "cw": np.ascontiguousarray(cw, dtype=np.float32),
        "cb": np.ascontiguousarray(cb, dtype=np.float32),
        "scal": np.ascontiguousarray(np.broadcast_to(sc[None, :], (128, 8))),
        "w2": np.ascontiguousarray(P["gla_gate_w2"][L][:, 128 * hg:128 * hg + 128]),
        "gb": np.ascontiguousarray(P["gla_gate_b"][L][128 * hg:128 * hg + 128].reshape(128, 1)),
        "consts": consts,
    }


NWT = 512
RING = 8
G_PRE, G_POSTMIX, G_PREMLP, G_POSTMLP, G_PLEPRE, G_PLEPOST = range(6)


def build_dense(TC, NT=512):
    nc = bass.Bass("TRN2", target_bir_lowering=False)
    NTT = TC // NT

    def din(name, shape):
        return nc.dram_tensor(name, list(shape), F32, kind="ExternalInput").ap()

    xT_d = din("xT", [128, KC, TC])
    odn_d = din("odnT", [128, 8, TC])
    ossm_d = din("ossmT", [128, 8, TC])
    ogla_d = din("oglaT", [128, 8, TC])
    pT_d = din("pT", [128, 2, TC])
    ws_d = din("wstream", [NWT, 128, 8, 128])
    wpp_d = din("wpp", [128, 2, D])
    gains_d = din("gains", [128, 6, KC])
    bn_d = din("bnorm", [128, 11])
    const_d = din("consts", [128, NCONST, 128])
    out_d = nc.dram_tensor("xoT", [128, KC, TC], F32, kind="ExternalOutput").ap()

    with ExitStack() as es:
        fw = FW(nc, es)
        sb, mm, act, ts, stt, tt, cp = fw.sb, fw.mm, fw.act, fw.ts, fw.stt, fw.tt, fw.cp
        consts = sb("consts", [128, NCONST, 128])
        gains = sb("gains", [128, 6, KC])
        bn = sb("bn", [128, 11])
        wpp = sb("wpp", [128, 2, D], BF16)
        fw.dma("sp", consts[:], const_d)
        fw.dma("sp", gains[:], gains_d)
        fw.dma("sp", bn[:], bn_d)
        fw.dma("pool", wpp[:], wpp_d)
        ones = consts[:, C_ONES, :]
        ones_bf = sb("ones_bf", [128, 128], BF16)
        cp(ones_bf[:], ones, e="dve")

        xs = sb("xs", [128, KC, NT])
        hT = sb("hT", [128, KC, NT], BF16)
        big = sb("big", [128, KC, NT])
        mp = sb("mp", [128, KC, NT], BF16)
        arena = es.enter_context(nc.sbuf_tensor("s_arena", [128, 32, NT], BF16))
        yb = [TT(arena[:, 8 * b:8 * b + 8, :], "yb%d" % b) for b in range(3)]
        upT = TT(arena[:, :, :], "upT")
        ring = [sb("ring%d" % i, [128, 8, 128], BF16) for i in range(RING)]
        pn = fw.ps("pn")
        pss = fw.ps("pss")
        pacc = [fw.ps("pacc%d" % i) for i in range(6)]
        pi = [0]

        def pbank():
            p = pacc[pi[0] % len(pacc)]
            pi[0] += 1
            return p

        bigsub = [big.sub((slice(None), k, slice(None)), "bigsub%d" % k) for k in range(KC)]
        scr_i = {"yz": [0, bigsub[0:4]], "ot": [0, bigsub[4:8]], "sg": [0, bigsub[8:12]], "t1": [0, bigsub[12:14]]}

        def scr(name):
            ent = scr_i[name]
            t = ent[1][ent[0] % len(ent[1])]
            ent[0] += 1
            return t

        tmp_i = {}

        def tmp(name, shape=(128, NT), n=2, dtype=F32):
            if name not in tmp_i:
                tmp_i[name] = [0, [sb("%s_%d" % (name, i), shape, dtype) for i in range(n)]]
            ent = tmp_i[name]
            t = ent[1][ent[0] % n]
            ent[0] += 1
            return t

        wstate = {"issued": 0, "next": 0, "total": NWT * NTT}

        def get_w():
            idx = wstate["next"]
            while wstate["issued"] < min(idx + RING, wstate["total"]):
                k = wstate["issued"]
                fw.dma("pool", ring[k % RING][:], ws_d[k % NWT])
                wstate["issued"] += 1
            wstate["next"] += 1
            return ring[idx % RING]

        def proj(out_ps, rhs_of_kc, nk):
            w = None
            for kc in range(nk):
                if kc % 8 == 0:
                    w = get_w()
                mm(out_ps, w[:, kc % 8, :], rhs_of_kc(kc), start=(kc == 0), stop=(kc == nk - 1))

        def rms_rstd(blocks, nfeat, dst, f32=False):
            n = len(blocks)
            for i, v in enumerate(blocks):
                if f32:
                    sq = tmp("sqf")
                    act(sq[:], v, AF.Square)
                    mm(pn[:, 0:NT], ones, sq[:], start=(i == 0), stop=(i == n - 1))
                else:
                    sq = tmp("sqb", dtype=BF16)
                    act(sq[:], v, AF.Square)
                    mm(pn[:, 0:NT], ones_bf[:], sq[:], start=(i == 0), stop=(i == n - 1))
            lt = tmp("lnt", n=1)
            act(lt[:], pn[:, 0:NT], AF.Ln, bias=EPS, scale=1.0 / nfeat)
            act(dst[:], lt[:], AF.Exp, scale=-0.5)

        def norm_to_hT(gi):
            rs = tmp("rstd", n=1)
            rms_rstd([xs[:, kc, :] for kc in range(KC)], D, rs)
            for kc in range(KC):
                stt(hT[:, kc, :], xs[:, kc, :], gains[:, gi, kc:kc + 1], rs[:], ALU.mult, ALU.mult)

        def residual_add(gi):
            rs = tmp("rstd", n=1)
            rms_rstd([big[:, kc, :] for kc in range(KC)], D, rs)
            for kc in range(KC):
                t1 = tmp("resid", n=1)
                stt(t1[:], big[:, kc, :], gains[:, gi, kc:kc + 1], rs[:], ALU.mult, ALU.mult)
                tt(xs[:, kc, :], xs[:, kc, :], t1[:], ALU.add)

        for ti in range(NTT):
            t0 = ti * NT
            tsl = slice(t0, t0 + NT)
            fw.dma("sp", xs[:], xT_d[:, :, tsl])
            norm_to_hT(G_PRE)
            if ti > 0:
                fw.alias_barrier(yb, [upT])
                fw.alias_barrier(bigsub, [big])
            for blk in range(8):
                ot = scr("ot")
                fw.dma("sp", ot[:], odn_d[:, blk, tsl])
                sq = tmp("sqf")
                act(sq[:], ot[:], AF.Square)
                mm(pss[:, 0:NT], ones, sq[:])
                lt = tmp("lnt2", n=1)
                act(lt[:], pss[:, 0:NT], AF.Ln, bias=EPS, scale=1.0 / 128)
                rs = tmp("rs2", n=1)
                act(rs[:], lt[:], AF.Exp, scale=-0.5)
                pg = pbank()
                proj(pg[:, 0:NT], lambda kc: hT[:, kc, :], KC)
                sg = scr("sg")
                act(sg[:], pg[:, 0:NT], AF.Silu)
                t1 = scr("t1")
                stt(t1[:], ot[:], bn[:, 0:1], rs[:], ALU.mult, ALU.mult)
                tt(yb[0][:, blk, :], t1[:], sg[:], ALU.mult)
            for grp in range(2):
                yz = []
                for j in range(4):
                    blk = 4 * grp + j
                    ot = scr("ot")
                    fw.dma("sp", ot[:], ossm_d[:, blk, tsl])
                    pg = pbank()
                    proj(pg[:, 0:NT], lambda kc: hT[:, kc, :], KC)
                    sg = scr("sg")
                    act(sg[:], pg[:, 0:NT], AF.Silu)
                    y = scr("yz")
                    tt(y[:], ot[:], sg[:], ALU.mult)
                    sq = tmp("sqf")
                    act(sq[:], y[:], AF.Square)
                    mm(pss[:, 0:NT], ones, sq[:], start=(j == 0), stop=(j == 3))
                    yz.append(y)
                lt = tmp("lnt2", n=1)
                act(lt[:], pss[:, 0:NT], AF.Ln, bias=EPS, scale=1.0 / 512)
                rs = tmp("rs2", n=1)
                act(rs[:], lt[:], AF.Exp, scale=-0.5)
                for j in range(4):
                    blk = 4 * grp + j
                    stt(yb[1][:, blk, :], yz[j][:], bn[:, 1 + blk:2 + blk], rs[:], ALU.mult, ALU.mult)
            for hd in range(4):
                ots, sgs = [], []
                for j in range(2):
                    blk = 2 * hd + j
                    ot = scr("ot")
                    fw.dma("sp", ot[:], ogla_d[:, blk, tsl])
                    sq = tmp("sqf")
                    act(sq[:], ot[:], AF.Square)
                    mm(pss[:, 0:NT], ones, sq[:], start=(j == 0), stop=(j == 1))
                    pg = pbank()
                    proj(pg[:, 0:NT], lambda kc: hT[:, kc, :], KC)
                    sg = scr("sg")
                    act(sg[:], pg[:, 0:NT], AF.Silu)
                    ots.append(ot)
                    sgs.append(sg)
                lt = tmp("lnt2", n=1)
                act(lt[:], pss[:, 0:NT], AF.Ln, bias=EPS, scale=1.0 / 256)
                rs = tmp("rs2", n=1)
                act(rs[:], lt[:], AF.Exp, scale=-0.5)
                for j in range(2):
                    blk = 2 * hd + j
                    t1 = scr("t1")
                    stt(t1[:], ots[j][:], bn[:, 9 + j:10 + j], rs[:], ALU.mult, ALU.mult)
                    tt(yb[2][:, blk, :], t1[:], sgs[j][:], ALU.mult)
            for dblk in range(KC):
                macc = tmp("macc")
                for b in range(3):
                    pg = pbank()
                    proj(pg[:, 0:NT], lambda kc: hT[:, kc, :], KC)
                    sgm = tmp("sgm", n=2)
                    act(sgm[:], pg[:, 0:NT], AF.Sigmoid)
                    pu = pbank()
                    proj(pu[:, 0:NT], lambda kc, b=b: yb[b][:, kc, :], 8)
                    if b == 0:
                        tt(macc[:], pu[:, 0:NT], sgm[:], ALU.mult)
                    else:
                        t2 = tmp("t2", n=1)
                        tt(t2[:], pu[:, 0:NT], sgm[:], ALU.mult)
                        if b == 1:
                            tt(macc[:], macc[:], t2[:], ALU.add)
                        else:
                            tt(mp[:, dblk, :], macc[:], t2[:], ALU.add)
            fw.alias_barrier([big], bigsub)
            for dblk in range(KC):
                po = pbank()
                proj(po[:, 0:NT], lambda kc: mp[:, kc, :], KC)
                cp(big[:, dblk, :], po[:, 0:NT])
            residual_add(G_POSTMIX)
            norm_to_hT(G_PREMLP)
            fw.alias_barrier([upT], yb)
            for half in range(2):
                for fb in range(32):
                    pu = pbank()
                    proj(pu[:, 0:NT], lambda kc: hT[:, kc, :], KC)
                    r = tmp("relu")
                    act(r[:], pu[:, 0:NT], AF.Relu)
                    tt(upT[:, fb, :], r[:], r[:], ALU.mult)
                for dblk in range(KC):
                    pd = pbank()
                    proj(pd[:, 0:NT], lambda kc: upT[:, kc, :], 32)
                    if half == 0:
                        cp(big[:, dblk, :], pd[:, 0:NT])
                    else:
                        tt(big[:, dblk, :], pd[:, 0:NT], big[:, dblk, :], ALU.add)
            residual_add(G_POSTMLP)
            norm_to_hT(G_PLEPRE)
            pf = tmp("pf", (128, 2, NT), n=1)
            fw.dma("sp", pf[:], pT_d[:, :, tsl])
            pb = tmp("pb", (128, 2, NT), n=1, dtype=BF16)
            cp(pb[:], pf[:], e="dve")
            for dblk in range(KC):
                pg = pbank()
                proj(pg[:, 0:NT], lambda kc: hT[:, kc, :], KC)
                sgm = tmp("sgm", n=2)
                act(sgm[:], pg[:, 0:NT], AF.Sigmoid)
                pe_ = pbank()
                for c in range(2):
                    mm(pe_[:, 0:NT], wpp[:, c, dblk * 128:dblk * 128 + 128], pb[:, c, :], start=(c == 0), stop=(c == 1))
                tt(big[:, dblk, :], pe_[:, 0:NT], sgm[:], ALU.mult)
            residual_add(G_PLEPOST)
            fw.dma("sp", out_d[:, :, tsl], xs[:])
        assert wstate["next"] == wstate["total"], (wstate, NWT)
        fw.finish()
        print("dense sbuf bytes/partition", fw.sb_bytes + 32 * NT * 2, "instr", fw.cnt)
    return nc


def half_tile(W, r0, c0):
    return W[r0:r0 + 1024, c0:c0 + 128].reshape(8, 128, 128).transpose(1, 0, 2)


def build_wstream(P, L):
    w_in, wb, wo = P["w_in"][L], P["w_branch"][L], P["w_out"][L]
    wu, wd, wg = P["w_up"][L], P["w_down"][L], P["w_ple_gate"][L]
    ws = np.empty((NWT, 128, 8, 128), np.float32)
    i = 0

    def put(W, r0, c0):
        nonlocal i
        ws[i] = half_tile(W, r0, c0)
        i += 1

    for off in (O_DNG, O_SZ, O_GO):
        for blk in range(8):
            put(w_in, 0, off + blk * 128)
            put(w_in, 1024, off + blk * 128)
    for dblk in range(16):
        for b in range(3):
            put(w_in, 0, O_BR + b * D + dblk * 128)
            put(w_in, 1024, O_BR + b * D + dblk * 128)
            put(wb[b], 0, dblk * 128)
    for dblk in range(16):
        put(wo, 0, dblk * 128)
        put(wo, 1024, dblk * 128)
    for half in range(2):
        for fb in range(32):
            put(wu, 0, (half * 32 + fb) * 128)
            put(wu, 1024, (half * 32 + fb) * 128)
        for dblk in range(16):
            for q in range(4):
                put(wd, half * 4096 + q * 1024, dblk * 128)
    for dblk in range(16):
        put(wg, 0, dblk * 128)
        put(wg, 1024, dblk * 128)
    assert i == NWT, i
    return ws


def tok_to_fm(a):
    Tn, C = a.shape
    return np.ascontiguousarray(a.reshape(Tn, C // 128, 128).transpose(2, 1, 0))


def fm_to_tok(a):
    p, nb, Tn = a.shape
    return np.ascontiguousarray(a.transpose(2, 1, 0).reshape(Tn, nb * 128))


def dense_shared_inputs(L, P, consts):
    gains = np.stack([P[k][L].reshape(KC, 128).T for k in
                      ("pre_mix_norm", "post_mix_norm", "pre_mlp_norm", "post_mlp_norm", "ple_pre_norm", "ple_post_norm")], axis=1)
    bn = np.concatenate([P["dn_norm"][L].reshape(128, 1), P["ssm_norm"][L].reshape(8, 128).T,
                         P["gla_norm"][L].reshape(2, 128).T], axis=1)
    return {
        "wstream": build_wstream(P, L),
        "wpp": fm_layout(P["w_ple_proj"][L]),
        "gains": np.ascontiguousarray(gains, dtype=np.float32),
        "bnorm": np.ascontiguousarray(bn, dtype=np.float32),
        "consts": consts,
    }


SEQ = 8192
NCORE = 8
TCORE = SEQ // NCORE
_PROG = {}


def _prog(kind):
    if kind not in _PROG:
        _PROG[kind] = build_mixer(SEQ) if kind == "mixer" else build_dense(TCORE)
    return _PROG[kind]


def kernel(**inputs):
    P = {k: np.asarray(v) for k, v in inputs.items()}
    x = np.ascontiguousarray(P["x"][0], dtype=np.float32)
    consts = make_consts()
    cores = list(range(NCORE))
    for L in range(2):
        xT_l = tok_to_fm(x)
        in_maps = [mixer_core_inputs(c, L, xT_l, P, consts) for c in cores]
        res = run_bass_kernel_spmd(_prog("mixer"), in_maps, core_ids=cores).results
        o_dn = np.concatenate([res[c]["o_dn"] for c in cores], axis=1)
        o_ssm = np.concatenate([res[c]["o_ssm"] for c in cores], axis=1)
        o_gla = np.concatenate([res[c]["o_gla"] for c in cores], axis=1)
        del res, in_maps
        sh = dense_shared_inputs(L, P, consts)
        in_maps = []
        for c in cores:
            ts_ = slice(c * TCORE, (c + 1) * TCORE)
            im = dict(sh)
            im.update({"xT": tok_to_fm(x[ts_]), "odnT": tok_to_fm(o_dn[ts_]), "ossmT": tok_to_fm(o_ssm[ts_]),
                       "oglaT": tok_to_fm(o_gla[ts_]), "pT": tok_to_fm(np.ascontiguousarray(P["p"][L][0, ts_]))})
            in_maps.append(im)
        res = run_bass_kernel_spmd(_prog("dense"), in_maps, core_ids=cores).results
        x = np.concatenate([fm_to_tok(res[c]["xoT"]) for c in cores], axis=0)
        del res, in_maps, sh
    return x[None].astype(np.float32)
```

```python
import numpy as np
from contextlib import ExitStack
import concourse.bass as bass
import concourse.mybir as mybir
from concourse.bass_utils import run_bass_kernel_spmd

F32 = mybir.dt.float32
BF16 = mybir.dt.bfloat16
ALU = mybir.AluOpType
AF = mybir.ActivationFunctionType

D = 2048
KC = 16
EPS = 1e-6
NEG = -30000.0
PE_ = "dve"


class Trk:
    __slots__ = ("w", "r", "dsem", "dcnt", "name", "excl")

    def __init__(self, name):
        self.excl = False
        self.w = None
        self.r = {}
        self.dsem = None
        self.dcnt = 0
        self.name = name


class V:
    __slots__ = ("ap", "trk")

    def __init__(self, ap, trk):
        self.ap = ap
        self.trk = trk

    def __getitem__(self, idx):
        return V(self.ap[idx], self.trk)


class TT:
    def __init__(self, ap, name):
        self.ap = ap
        self.trk = Trk(name)

    def __getitem__(self, idx):
        return V(self.ap[idx], self.trk)

    def view(self, idx):
        t = TT(self.ap[idx], self.trk.name)
        t.trk = self.trk
        return t

    def sub(self, idx, name=None):
        return TT(self.ap[idx], name or self.trk.name + "_sub")


class FW:
    def __init__(self, nc, es):
        self.nc = nc
        self.es = es
        self.eng = {"pe": nc.tensor, "dve": nc.vector, "act": nc.scalar, "pool": nc.gpsimd, "sp": nc.sync}
        self.sem = {}
        self.cnt = {}
        self.waited = {}
        for e in self.eng:
            self.sem[e] = es.enter_context(nc.semaphore("sem_" + e))
            self.cnt[e] = 0
            self.waited[e] = {}
        self.semowner = {id(self.sem[e]): e for e in self.eng}
        self.nsem = len(self.eng)
        self.out_dmas = []
        self.uid = 0

    def sb(self, name, shape, dtype=F32):
        n = 1
        for d in shape[1:]:
            n *= d
        self.sb_bytes = getattr(self, "sb_bytes", 0) + n * (2 if dtype == BF16 else 4)
        return TT(self.es.enter_context(self.nc.sbuf_tensor("s_" + name, list(shape), dtype)), name)

    def ps(self, name, shape=(128, 512), dtype=F32):
        t = TT(self.es.enter_context(self.nc.psum_tensor("p_" + name, list(shape), dtype)), name)
        t.trk.excl = True
        return t

    def _wait(self, e, dep):
        sem, val = dep
        k = id(sem)
        if self.semowner.get(k) == e and e == "pe":
            return
        if self.waited[e].get(k, 0) >= val:
            return
        self.eng[e].wait_ge(sem, val)
        self.waited[e][k] = val

    def _sync(self, e, reads, writes):
        for t in reads:
            if t.w is not None:
                self._wait(e, t.w)
        for t in writes:
            if t.w is not None:
                self._wait(e, t.w)
            for k, dep in t.r.items():
                self._wait(e, dep)

    def op(self, e, name, **kw):
        reads, writes = [], []
        args = {}
        for k, v in kw.items():
            if isinstance(v, V):
                (writes if (k in ("out", "accum_out") or v.trk.excl) else reads).append(v.trk)
                args[k] = v.ap
            else:
                args[k] = v
        self._sync(e, reads, writes)
        inst = getattr(self.eng[e], name)(**args)
        self.cnt[e] += 1
        inst.then_inc(self.sem[e], 1)
        me = (self.sem[e], self.cnt[e])
        for t in reads:
            t.r[id(self.sem[e])] = me
        for t in writes:
            t.w = me
            t.r = {}
        return inst

    def dma(self, e, out, in_, **kw):
        if isinstance(out, V):
            t = out.trk
            self._sync(e, [], [t])
            if t.dsem is None:
                t.dsem = self.es.enter_context(self.nc.semaphore("dsem%d" % self.nsem))
                self.nsem += 1
            self.eng[e].dma_start(out=out.ap, in_=in_, **kw).then_inc(t.dsem, 16)
            t.dcnt += 16
            t.w = (t.dsem, t.dcnt)
            t.r = {}
        else:
            t = in_.trk
            self._sync(e, [t], [])
            if t.dsem is None:
                t.dsem = self.es.enter_context(self.nc.semaphore("dsem%d" % self.nsem))
                self.nsem += 1
            self.eng[e].dma_start(out=out, in_=in_.ap, **kw).then_inc(t.dsem, 16)
            t.dcnt += 16
            t.r[id(t.dsem)] = (t.dsem, t.dcnt)
            self.out_dmas.append((t.dsem, t.dcnt))

    def alias_barrier(self, new, old):
        deps = {}
        for t in old:
            for dep in ([t.trk.w] if t.trk.w is not None else []) + list(t.trk.r.values()):
                k = id(dep[0])
                if k not in deps or deps[k][1] < dep[1]:
                    deps[k] = dep
        for t in new:
            for k, dep in deps.items():
                if k not in t.trk.r or t.trk.r[k][1] < dep[1]:
                    t.trk.r[k] = dep

    def finish(self):
        last = {}
        for sem, val in self.out_dmas:
            last[id(sem)] = (sem, val)
        for sem, val in last.values():
            self._wait("sp", (sem, val))

    def mm(self, out, lhsT, rhs, start=True, stop=True):
        return self.op("pe", "matmul", out=out, lhsT=lhsT, rhs=rhs, start=start, stop=stop)

    def tr(self, out, in_, ident):
        return self.op("pe", "transpose", out=out, in_=in_, identity=ident)

    def act(self, out, in_, func, bias=None, scale=None, e="act"):
        kw = {}
        if bias is not None:
            kw["bias"] = bias
        if scale is not None:
            kw["scale"] = scale
        return self.op(e, "activation", out=out, in_=in_, func=func, **kw)

    def ts(self, out, in0, s1, op0, s2=None, op1=None, e="dve"):
        if op1 is None:
            return self.op(e, "tensor_scalar", out=out, in0=in0, scalar1=s1, scalar2=None, op0=op0)
        return self.op(e, "tensor_scalar", out=out, in0=in0, scalar1=s1, scalar2=s2, op0=op0, op1=op1)

    def stt(self, out, in0, scalar, in1, op0, op1):
        return self.op("dve", "scalar_tensor_tensor", out=out, in0=in0, scalar=scalar, in1=in1, op0=op0, op1=op1)

    def tt(self, out, in0, in1, op, e="dve"):
        return self.op(e, "tensor_tensor", out=out, in0=in0, in1=in1, op=op)

    def cp(self, out, in_, e="act"):
        if e == "act":
            return self.op("act", "copy", out=out, in_=in_)
        return self.op(e, "tensor_copy", out=out, in_=in_)


C_ID, C_ONES, C_TRI, C_BLK, C_NMT, C_PMS, C_M01, C_RST = range(8)
NCONST = 8


def make_consts():
    p = np.arange(128)[:, None]
    f = np.arange(128)[None, :]
    same = (p // 64) == (f // 64)
    c = np.zeros((128, NCONST, 128), np.float32)
    c[:, C_ID] = (p == f)
    c[:, C_ONES] = 1.0
    c[:, C_TRI] = (same & (p <= f))
    c[:, C_BLK] = same
    c[:, C_NMT] = np.where(same & (f >= p), 0.0, NEG)
    c[:, C_PMS] = np.where(same & (p > f), 0.0, -NEG)
    c[:, C_M01] = (same & (f >= p))
    c[:, C_RST] = ((f % 64) != 0)
    return c


NFM = 8 * 128 + 16
NTM = 132
S_DN_ALOG, S_DN_DTB, S_S_DTB0, S_S_DTB1, S_S_ALOG0, S_S_ALOG1, S_S_D0, S_S_D1 = range(8)


def build_mixer(T, ST=256, parts=("dn", "ssm", "gla"), dn_stage=99):
    nc = bass.Bass("TRN2", target_bir_lowering=False)
    NST = T // ST
    NSUB = ST // 128

    def din(name, shape):
        return nc.dram_tensor(name, list(shape), F32, kind="ExternalInput").ap()

    xT_d = din("xT", [128, KC, T])
    gain_d = din("gain", [128, KC])
    wfm_d = din("wfm", [128, KC, NFM])
    wtm_d = din("wtm", [128, KC, NTM])
    cw_d = din("cw", [128, 6, 4])
    cb_d = din("cb", [128, 3])
    scal_d = din("scal", [128, 8])
    w2_d = din("w2", [16, 128])
    gb_d = din("gb", [128, 1])
    const_d = din("consts", [128, NCONST, 128])
    odn_d = nc.dram_tensor("o_dn", [T, 128], F32, kind="ExternalOutput").ap()
    ossm_d = nc.dram_tensor("o_ssm", [T, 128], F32, kind="ExternalOutput").ap()
    ogla_d = nc.dram_tensor("o_gla", [T, 128], F32, kind="ExternalOutput").ap()

    with ExitStack() as es:
        fw = FW(nc, es)
        sb, mm, tr, act, ts, stt, tt, cp = fw.sb, fw.mm, fw.tr, fw.act, fw.ts, fw.stt, fw.tt, fw.cp

        wfm = sb("wfm", [128, KC, NFM], BF16)
        wtm = sb("wtm", [128, KC, NTM], BF16)
        gain = sb("gain", [128, KC])
        cw = sb("cw", [128, 6, 4])
        cb = sb("cb", [128, 3])
        scal = sb("scal", [128, 8])
        w2 = sb("w2", [16, 128])
        gb = sb("gb", [128, 1])
        consts = sb("consts", [128, NCONST, 128])
        fw.dma("sp", consts[:], const_d)
        fw.dma("sp", gain[:], gain_d)
        fw.dma("sp", cw[:], cw_d)
        fw.dma("sp", cb[:], cb_d)
        fw.dma("sp", scal[:], scal_d)
        fw.dma("sp", w2[:], w2_d)
        fw.dma("sp", gb[:], gb_d)
        wparts = []
        for g in range(4):
            wp = wfm.sub((slice(None), slice(g * 4, g * 4 + 4), slice(None)), "wfm%d" % g)
            fw.dma("pool", wp[:], wfm_d[:, g * 4:g * 4 + 4, :])
            wparts.append(wp)
        fw.dma("pool", wtm[:], wtm_d)

        def wf(kc, c0, c1):
            return wparts[kc // 4][:, kc % 4, c0:c1]

        def cst(i):
            return consts[:, i, :]

        ident, ones, tri, blk = cst(C_ID), cst(C_ONES), cst(C_TRI), cst(C_BLK)
        nmt, pms, m01, rst = cst(C_NMT), cst(C_PMS), cst(C_M01), cst(C_RST)

        ones_bf = sb("ones_bf", [128, 128], BF16)
        cp(ones_bf[:], ones, e="dve")
        nega = sb("nega", [128, 4])
        act(nega[:, 0:1], scal[:, S_DN_ALOG:S_DN_ALOG + 1], AF.Exp)
        act(nega[:, 1:3], scal[:, S_S_ALOG0:S_S_ALOG1 + 1], AF.Exp)
        ts(nega[:, 0:3], nega[:, 0:3], -1.0, ALU.mult)
        ngb = sb("ngb", [128, 1])
        ts(ngb[:], gb[:], -1.0, ALU.mult)

        xt = [sb("xt%d" % i, [128, KC, ST]) for i in range(2)]
        sqb = [sb("sqb%d" % i, [128, ST], BF16) for i in range(2)]
        hT = [sb("hT%d" % i, [128, KC, ST], BF16) for i in range(2)]
        lnt = sb("lnt", [128, ST])
        rstd = sb("rstd", [128, ST])
        raw = [sb("raw%d" % b, [128, ST + 3]) for b in range(6)]
        cva = [sb("cva%d" % b, [128, ST]) for b in range(2)]
        cvq = [sb("cvq%d" % b, [128, ST]) for b in range(2)]
        cvo = [[sb("cvo%d_%d" % (i, b), [128, ST]) for b in range(4)] for i in range(2)]
        gq = [sb("gq%d" % i, [128, ST]) for i in range(2)]
        gk_ = [sb("gk%d" % i, [128, ST]) for i in range(2)]
        lrT = [sb("lrT%d" % i, [16, ST]) for i in range(2)]
        sqf = sb("sqf", [128, ST])
        qn = [sb("qn%d" % i, [128, ST]) for i in range(2)]
        kn = [sb("kn%d" % i, [128, ST]) for i in range(2)]
        tms = [[sb("tm%d_%d" % (i, s), [128, NTM]) for s in range(NSUB)] for i in range(2)]
        for b in range(6):
            ts(raw[b][:, 0:3], consts[:, C_ONES, 0:3], 0.0, ALU.mult)

        pacc = fw.ps("pacc")
        pnb = fw.ps("pnb")
        pn = pnb.view((slice(None), slice(0, 256)))
        ptm = pnb.view((slice(None), slice(256, 256 + NTM)))
        pools = {"dn": [fw.ps("pdn%d" % i) for i in range(3)], "ssm": [fw.ps("pss%d" % i) for i in range(2)],
                 "gla": [fw.ps("pgl%d" % i) for i in range(1)]}
        pool_i = {"dn": 0, "ssm": 0, "gla": 0}

        def pslot(which):
            lst = pools[which]
            b = lst[pool_i[which] % len(lst)]
            pool_i[which] += 1
            return b.view((slice(None), slice(0, 128)))

        tmp_i = {}

        def tmp(name, shape=(128, 128), n=2, dtype=F32):
            if name not in tmp_i:
                tmp_i[name] = [0, [sb("%s_%d" % (name, i), shape, dtype) for i in range(n)]]
            ent = tmp_i[name]
            t = ent[1][ent[0] % n]
            ent[0] += 1
            return t

        S_dn = sb("S_dn", [128, 128])
        S_ssm = sb("S_ssm", [128, 128])
        S_gla = sb("S_gla", [128, 128])
        for S in (S_dn, S_ssm, S_gla):
            ts(S[:], ones, 0.0, ALU.mult)

        KS = 128.0 ** -0.5

        def prologue(st):
            par = st % 2
            t0 = st * ST
            x = xt[par]
            h = hT[par]
            fw.dma("sp", x[:], xT_d[:, :, t0:t0 + ST])
            for kc in range(KC):
                sq = sqb[kc % 2]
                act(sq[:], x[:, kc, :], AF.Square)
                mm(pn[:, 0:ST], ones_bf[:], sq[:], start=(kc == 0), stop=(kc == KC - 1))
            yield
            act(lnt[:], pn[:, 0:ST], AF.Ln, bias=EPS, scale=1.0 / D)
            act(rstd[:], lnt[:], AF.Exp, scale=-0.5)
            for kc in range(KC):
                stt(h[:, kc, :], x[:, kc, :], gain[:, kc:kc + 1], rstd[:], ALU.mult, ALU.mult)
                if kc % 4 == 3:
                    yield
            for b in range(9):
                M = 128 if b < 8 else 16
                for kc in range(KC):
                    mm(pacc[0:M, 0:ST], wf(kc, b * 128, b * 128 + M), h[:, kc, :], start=(kc == 0), stop=(kc == KC - 1))
                if b < 6:
                    cp(raw[b][:, 3:3 + ST], pacc[:, 0:ST])
                elif b == 6:
                    cp(gq[par][:], pacc[:, 0:ST])
                elif b == 7:
                    cp(gk_[par][:], pacc[:, 0:ST])
                else:
                    cp(lrT[par][:], pacc[0:16, 0:ST])
                yield
            for b in range(6):
                ca = cva[b % 2]
                if b >= 3:
                    ts(ca[:], raw[b][:, 0:ST], cw[:, b, 0:1], ALU.mult, cb[:, b - 3:b - 2], ALU.add)
                else:
                    ts(ca[:], raw[b][:, 0:ST], cw[:, b, 0:1], ALU.mult)
                for k in range(1, 4):
                    stt(ca[:], raw[b][:, k:k + ST], cw[:, b, k:k + 1], ca[:], ALU.mult, ALU.add)
                dst = cvq[b] if b < 2 else cvo[par][b - 2]
                act(dst[:], ca[:], AF.Silu)
                cp(raw[b][:, 0:3], raw[b][:, ST:ST + 3], e="dve")
                yield
            for src, dst in ((cvq[0], qn[par]), (cvq[1], kn[par])):
                act(sqf[:], src[:], AF.Square)
                mm(pn[:, 0:ST], ones, sqf[:])
                act(lnt[:], pn[:, 0:ST], AF.Ln, bias=EPS)
                act(lnt[:], lnt[:], AF.Exp, scale=-0.5)
                tt(dst[:], src[:], lnt[:], ALU.mult)
                yield
            for s in range(NSUB):
                cs = slice(s * 128, s * 128 + 128)
                for kc in range(KC):
                    mm(ptm[:, 0:NTM], h[:, kc, cs], wtm[:, kc, :], start=(kc == 0), stop=(kc == KC - 1))
                cp(tms[par][s][:], ptm[:, 0:NTM])
                yield

        def dn_tile(st, s):
            par = st % 2
            cs = slice(s * 128, s * 128 + 128)
            tg = st * ST + s * 128
            tm = tms[par][s]
            qn_, kn_, vT = qn[par], kn[par], cvo[par][0]
            P = lambda: pslot("dn")
            sm = tmp("dn_sm", (128, 16))
            act(sm[:, 0:1], tm[:, 128:129], AF.Sigmoid)
            act(sm[:, 1:2], tm[:, 129:130], AF.Exp, bias=scal[:, S_DN_DTB:S_DN_DTB + 1])
            act(sm[:, 2:3], sm[:, 1:2], AF.Ln, bias=1.0)
            ts(sm[:, 3:4], sm[:, 2:3], nega[:, 0:1], ALU.mult)
            ts(sm[:, 4:5], sm[:, 0:1], -1.0, ALU.mult)
            Gbc = tmp("Gbc")
            ts(Gbc[:], ones, sm[:, 3:4], ALU.mult)
            p_gc = P()
            mm(p_gc[:], Gbc[:], tri)
            p_c = P()
            mm(p_c[:, 0:1], tri, sm[:, 3:4])
            mm(p_c[:, 1:2], blk, sm[:, 3:4])
            yield
            cp(sm[:, 5:7], p_c[:, 0:2], e="dve")
            ET = tmp("ET")
            stt(ET[:], p_gc[:], sm[:, 5:6], nmt, ALU.subtract, ALU.add)
            decT = tmp("decT")
            act(decT[:], ET[:], AF.Exp)
            ES = tmp("ES")
            stt(ES[:], p_gc[:], sm[:, 5:6], pms, ALU.subtract, ALU.add)
            decS = tmp("decS")
            act(decS[:], ES[:], AF.Exp, scale=-1.0)
            egbc = tmp("egbc")
            act(egbc[:], p_gc[:], AF.Exp)
            act(sm[:, 7:8], sm[:, 5:6], AF.Exp)
            tt(sm[:, 10:11], sm[:, 6:7], sm[:, 5:6], ALU.subtract)
            act(sm[:, 8:9], sm[:, 10:11], AF.Exp)
            tt(sm[:, 9:10], sm[:, 0:1], sm[:, 7:8], ALU.mult)
            p_kk = P()
            mm(p_kk[:], kn_[:, cs], kn_[:, cs])
            p_qk = P()
            mm(p_qk[:], kn_[:, cs], qn_[:, cs])
            yield
            Nm = tmp("Nm", n=3)
            stt(Nm[:], p_kk[:], sm[:, 4:5], decS[:], ALU.mult, ALU.mult)
            attnT = tmp("attnT")
            stt(attnT[:], p_qk[:], KS, decT[:], ALU.mult, ALU.mult)
            p_b = P()
            tr(p_b[:], Nm[:], ident)
            yield
            Bm = tmp("Bm", n=3)
            cp(Bm[:], p_b[:])
            R = tmp("R", n=3)
            tt(R[:], p_b[:], ident, ALU.add)
            curN, curB = Nm, Bm
            for step in range(5):
                p_n2 = P()
                mm(p_n2[:], curB[:], curN[:])
                if step < 4:
                    p_b2 = P()
                    mm(p_b2[:], curN[:], curB[:])
                yield
                N2 = tmp("Nm", n=3)
                cp(N2[:], p_n2[:])
                if step < 4:
                    B2 = tmp("Bm", n=3)
                    cp(B2[:], p_b2[:], e="dve")
                p_r = P()
                mm(p_r[:], N2[:], R[:])
                yield
                R2 = tmp("R", n=3)
                tt(R2[:], p_r[:], R[:], ALU.add)
                R = R2
                curN = N2
                if step < 4:
                    curB = B2
            p_kt = P()
            tr(p_kt[:], kn_[:, cs], ident)
            p_vt = P()
            tr(p_vt[:], vT[:, cs], ident)
            yield
            kbg = tmp("kbg")
            ts(kbg[:], p_kt[:], sm[:, 9:10], ALU.mult)
            kd = tmp("kd")
            ts(kd[:], p_kt[:], sm[:, 8:9], ALU.mult)
            bv = tmp("bv")
            ts(bv[:], p_vt[:], sm[:, 0:1], ALU.mult)
            p_w = P()
            mm(p_w[:], kbg[:], R[:])
            p_u = P()
            mm(p_u[:], R[:], bv[:])
            yield
            wT = tmp("wT")
            cp(wT[:], p_w[:])
            u = tmp("u")
            cp(u[:], p_u[:])
            qgT = tmp("qgT")
            stt(qgT[:], qn_[:, cs], KS, egbc[:], ALU.mult, ALU.mult)
            o_dn = tmp("o_dn")
            vnew = tmp("vnew")
            for c in range(2):
                r = slice(64 * c, 64 * c + 64)
                p1 = P()
                mm(p1[:], wT[:], S_dn[:])
                yield
                tt(vnew[r, :], u[r, :], p1[r, :], ALU.subtract)
                po = P()
                mm(po[:], qgT[:], S_dn[:], start=True, stop=False)
                mm(po[:], attnT[r, :], vnew[r, :], start=False, stop=True)
                p_s = P()
                mm(p_s[:], kd[r, :], vnew[r, :])
                yield
                cp(o_dn[r, :], po[r, :])
                gl = egbc[:, 64 * c + 63:64 * c + 64]
                stt(S_dn[:], S_dn[:], gl, p_s[:], ALU.mult, ALU.add)
            fw.dma("sp", odn_d[tg:tg + 128, :], o_dn[:])

        def ssm_tile(st, s):
            par = st % 2
            cs = slice(s * 128, s * 128 + 128)
            tg = st * ST + s * 128
            tm = tms[par][s]
            P = lambda: pslot("ssm")
            ss = tmp("ss_sm", (128, 16))
            act(ss[:, 0:1], tm[:, 130:131], AF.Exp, bias=scal[:, S_S_DTB0:S_S_DTB0 + 1])
            act(ss[:, 1:2], tm[:, 131:132], AF.Exp, bias=scal[:, S_S_DTB1:S_S_DTB1 + 1])
            act(ss[:, 2:4], ss[:, 0:2], AF.Ln, bias=1.0)
            tt(ss[:, 4:6], ss[:, 2:4], nega[:, 1:3], ALU.mult)
            p_c = P()
            mm(p_c[:, 0:2], tri, ss[:, 4:6])
            mm(p_c[:, 2:4], blk, ss[:, 4:6])
            yield
            cp(ss[:, 6:10], p_c[:, 0:4], e="dve")
            tt(ss[:, 12:14], ss[:, 8:10], ss[:, 6:8], ALU.subtract)
            act(ss[:, 10:12], ss[:, 12:14], AF.Exp)
            xT_s, BT_s, CT_s = cvo[par][1][:, cs], cvo[par][2][:, cs], cvo[par][3][:, cs]
            LTs, eab = [], []
            for h in range(2):
                Abc = tmp("Abc")
                ts(Abc[:], ones, ss[:, 4 + h:5 + h], ALU.mult)
                p_a = P()
                mm(p_a[:], Abc[:], tri)
                yield
                ETs = tmp("ETs")
                stt(ETs[:], p_a[:], ss[:, 6 + h:7 + h], nmt, ALU.subtract, ALU.add)
                LT = tmp("LT", n=4)
                act(LT[:], ETs[:], AF.Exp)
                ea = tmp("eab", n=4)
                act(ea[:], p_a[:], AF.Exp)
                LTs.append(LT)
                eab.append(ea)
            p_cb = P()
            mm(p_cb[:], BT_s, CT_s)
            yield
            WT, CgT = [], []
            for h in range(2):
                w_ = tmp("WT", n=4)
                tt(w_[:], p_cb[:], LTs[h][:], ALU.mult)
                cg = tmp("CgT", n=4)
                tt(cg[:], CT_s, eab[h][:], ALU.mult)
                WT.append(w_)
                CgT.append(cg)
            p_xt = P()
            tr(p_xt[:], xT_s, ident)
            yield
            xs = tmp("xs")
            cp(xs[:], p_xt[:])
            xdt = tmp("xdt")
            xdd = tmp("xdd")
            for h in range(2):
                hc = slice(64 * h, 64 * h + 64)
                ts(xdt[:, hc], p_xt[:, hc], ss[:, 2 + h:3 + h], ALU.mult)
                ts(xdd[:, hc], xdt[:, hc], ss[:, 10 + h:11 + h], ALU.mult)
            p_bt = P()
            tr(p_bt[:], BT_s, ident)
            yield
            Btok = tmp("Btok")
            cp(Btok[:], p_bt[:])
            y_ssm = tmp("y_ssm")
            for c in range(2):
                r = slice(64 * c, 64 * c + 64)
                po = P()
                for h in range(2):
                    hc = slice(64 * h, 64 * h + 64)
                    mm(po[:, hc], CgT[h][:], S_ssm[:, hc], start=True, stop=False)
                    mm(po[:, hc], WT[h][r, :], xdt[r, hc], start=False, stop=True)
                p_s = P()
                mm(p_s[:], Btok[r, :], xdd[r, :])
                yield
                for h in range(2):
                    hc = slice(64 * h, 64 * h + 64)
                    stt(y_ssm[r, hc], xs[r, hc], scal[r, S_S_D0 + h:S_S_D0 + h + 1], po[r, hc], ALU.mult, ALU.add)
                for h in range(2):
                    hc = slice(64 * h, 64 * h + 64)
                    stt(S_ssm[:, hc], S_ssm[:, hc], eab[h][:, 64 * c + 63:64 * c + 64], p_s[:, hc], ALU.mult, ALU.add)
            fw.dma("sp", ossm_d[tg:tg + 128, :], y_ssm[:])

        def gla_tile(st, s):
            par = st % 2
            cs = slice(s * 128, s * 128 + 128)
            tg = st * ST + s * 128
            tm = tms[par][s]
            P = lambda: pslot("gla")
            p_g = P()
            mm(p_g[:], w2[:], lrT[par][:, cs])
            yield
            eg = tmp("g_e")
            act(eg[:], p_g[:], AF.Exp, bias=ngb[:], scale=-1.0)
            lT = tmp("g_l")
            act(lT[:], eg[:], AF.Ln, bias=1.0)
            LcT = tmp("g_Lc")
            fw.op("dve", "tensor_tensor_scan", out=LcT[:], data0=rst, data1=lT[:], initial=0.0,
                  op0=ALU.mult, op1=ALU.add)
            gs = tmp("g_sm", (128, 4))
            ts(gs[:, 0:1], LcT[:, 63:64], -1.0 / 16.0, ALU.mult)
            ts(gs[:, 1:2], LcT[:, 127:128], -1.0 / 16.0, ALU.mult)
            egT = tmp("g_eg")
            act(egT[:], LcT[:], AF.Exp, scale=-1.0 / 16.0)
            engT = tmp("g_eng")
            act(engT[:], LcT[:], AF.Exp, scale=1.0 / 16.0)
            edT = tmp("g_ed")
            for c in range(2):
                act(edT[:, 64 * c:64 * c + 64], LcT[:, 64 * c:64 * c + 64], AF.Exp, bias=gs[:, c:c + 1], scale=1.0 / 16.0)
            yield
            qpT = tmp("g_qp")
            stt(qpT[:], gq[par][:, cs], KS, egT[:], ALU.mult, ALU.mult)
            kppT = tmp("g_kpp")
            tt(kppT[:], gk_[par][:, cs], engT[:], ALU.mult)
            kdT = tmp("g_kdT")
            tt(kdT[:], gk_[par][:, cs], edT[:], ALU.mult)
            p_kd = P()
            tr(p_kd[:], kdT[:], ident)
            yield
            kdg = tmp("g_kd")
            cp(kdg[:], p_kd[:])
            p_at = P()
            mm(p_at[:], kppT[:], qpT[:])
            yield
            atg = tmp("g_at")
            tt(atg[:], p_at[:], m01, ALU.mult)
            o_gla = tmp("o_gla")
            vt = tm[:, 0:128]
            for c in range(2):
                r = slice(64 * c, 64 * c + 64)
                po = P()
                mm(po[:], qpT[:], S_gla[:], start=True, stop=False)
                mm(po[:], atg[r, :], vt[r, :], start=False, stop=True)
                yield
                cp(o_gla[r, :], po[r, :])
                p_s = P()
                mm(p_s[:], kdg[r, :], vt[r, :])
                yield
                stt(S_gla[:], S_gla[:], egT[:, 64 * c + 63:64 * c + 64], p_s[:], ALU.mult, ALU.add)
            fw.dma("sp", ogla_d[tg:tg + 128, :], o_gla[:])

        def exhaust(g):
            for _ in g:
                pass

        exhaust(prologue(0))
        for st in range(NST):
            bg = prologue(st + 1) if st + 1 < NST else None
            for s in range(NSUB):
                live = []
                if "dn" in parts:
                    live.append(dn_tile(st, s))
                if "ssm" in parts:
                    live.append(ssm_tile(st, s))
                if "gla" in parts:
                    live.append(gla_tile(st, s))
                while live:
                    for g in list(live):
                        try:
                            next(g)
                        except StopIteration:
                            live.remove(g)
                    if bg is not None:
                        try:
                            next(bg)
                        except StopIteration:
                            bg = None
            if bg is not None:
                exhaust(bg)
        fw.finish()
        print("mixer sbuf bytes/partition", fw.sb_bytes, "instr", fw.cnt)
    return nc


O_DNQ, O_DNK, O_DNV, O_DNB, O_DNA, O_DNG = 0, 1024, 2048, 3072, 3080, 3088
O_SZ, O_SX, O_SB, O_SC, O_SDT = 4112, 5136, 6160, 6416, 6672
O_GQ, O_GK, O_GV, O_GLR, O_GO, O_BR = 6688, 7200, 7712, 8736, 8752, 9776
IN_TOTAL = 15920


def fm_layout(a):
    R, C = a.shape
    return np.ascontiguousarray(a.reshape(R // 128, 128, C).transpose(1, 0, 2))


def mixer_core_inputs(c, L, xT_l, P, consts):
    w_in = P["w_in"][L]
    g = c // 4
    hg, half = c // 2, c % 2
    fm_cols = np.concatenate([
        np.arange(O_DNQ + 128 * c, O_DNQ + 128 * c + 128),
        np.arange(O_DNK + 128 * c, O_DNK + 128 * c + 128),
        np.arange(O_DNV + 128 * c, O_DNV + 128 * c + 128),
        np.arange(O_SX + 128 * c, O_SX + 128 * c + 128),
        np.arange(O_SB + 128 * g, O_SB + 128 * g + 128),
        np.arange(O_SC + 128 * g, O_SC + 128 * g + 128),
        np.arange(O_GQ + 128 * hg, O_GQ + 128 * hg + 128),
        np.arange(O_GK + 128 * hg, O_GK + 128 * hg + 128),
        np.arange(O_GLR, O_GLR + 16)])
    tm_cols = np.concatenate([
        np.arange(O_GV + 256 * hg + 128 * half, O_GV + 256 * hg + 128 * half + 128),
        [O_DNB + c, O_DNA + c, O_SDT + 2 * c, O_SDT + 2 * c + 1]]).astype(np.int64)
    dcw = P["dn_conv_w"][L]
    scw = P["ssm_conv_w"][L]
    scb = P["ssm_conv_b"][L]
    chans = [(dcw, 128 * c), (dcw, 1024 + 128 * c), (dcw, 2048 + 128 * c),
             (scw, 128 * c), (scw, 1024 + 128 * g), (scw, 1280 + 128 * g)]
    cw = np.stack([w[:, o:o + 128].T for (w, o) in chans], axis=1)
    cb = np.stack([scb[o:o + 128] for o in (128 * c, 1024 + 128 * g, 1280 + 128 * g)], axis=1)
    sc = np.array([P["dn_a_log"][L][c], P["dn_dt_bias"][L][c],
                   P["ssm_dt_bias"][L][2 * c], P["ssm_dt_bias"][L][2 * c + 1],
                   P["ssm_a_log"][L][2 * c], P["ssm_a_log"][L][2 * c + 1],
                   P["ssm_d"][L][2 * c], P["ssm_d"][L][2 * c + 1]], np.float32)
    return {
        "xT": xT_l,
        "gain": np.ascontiguousarray(P["pre_mix_norm"][L].reshape(KC, 128).T),
        "wfm": fm_layout(w_in[:, fm_cols]),
        "wtm": fm_layout(w_in[:, tm_cols]),
        "cw": np.ascontiguousarray(cw, dtype=np.float32),
        "cb": np.ascontiguousarray(cb, dtype=np.float32),
        "scal": np.ascontiguousarray(np.broadcast_to(sc[None, :], (128, 8))),
        "w2": np.ascontiguousarray(P["gla_gate_w2"][L][:, 128 * hg:128 * hg + 128]),
        "gb": np.ascontiguousarray(P["gla_gate_b"][L][128 * hg:128 * hg + 128].reshape(128, 1)),
        "consts": consts,
    }


NWT = 512
RING = 8
G_PRE, G_POSTMIX, G_PREMLP, G_POSTMLP, G_PLEPRE, G_PLEPOST = range(6)


def build_dense(TC, NT=512):
    nc = bass.Bass("TRN2", target_bir_lowering=False)
    NTT = TC // NT

    def din(name, shape):
        return nc.dram_tensor(name, list(shape), F32, kind="ExternalInput").ap()

    xT_d = din("xT", [128, KC, TC])
    odn_d = din("odnT", [128, 8, TC])
    ossm_d = din("ossmT", [128, 8, TC])
    ogla_d = din("oglaT", [128, 8, TC])
    pT_d = din("pT", [128, 2, TC])
    ws_d = din("wstream", [NWT, 128, 8, 128])
    wpp_d = din("wpp", [128, 2, D])
    gains_d = din("gains", [128, 6, KC])
    bn_d = din("bnorm", [128, 11])
    const_d = din("consts", [128, NCONST, 128])
    out_d = nc.dram_tensor("xoT", [128, KC, TC], F32, kind="ExternalOutput").ap()

    with ExitStack() as es:
        fw = FW(nc, es)
        sb, mm, act, ts, stt, tt, cp = fw.sb, fw.mm, fw.act, fw.ts, fw.stt, fw.tt, fw.cp
        consts = sb("consts", [128, NCONST, 128])
        gains = sb("gains", [128, 6, KC])
        bn = sb("bn", [128, 11])
        wpp = sb("wpp", [128, 2, D], BF16)
        fw.dma("sp", consts[:], const_d)
        fw.dma("sp", gains[:], gains_d)
        fw.dma("sp", bn[:], bn_d)
        fw.dma("pool", wpp[:], wpp_d)
        ones = consts[:, C_ONES, :]
        ones_bf = sb("ones_bf", [128, 128], BF16)
        cp(ones_bf[:], ones, e="dve")

        xs = sb("xs", [128, KC, NT])
        hT = sb("hT", [128, KC, NT], BF16)
        big = sb("big", [128, KC, NT])
        mp = sb("mp", [128, KC, NT], BF16)
        arena = es.enter_context(nc.sbuf_tensor("s_arena", [128, 32, NT], BF16))
        yb = [TT(arena[:, 8 * b:8 * b + 8, :], "yb%d" % b) for b in range(3)]
        upT = TT(arena[:, :, :], "upT")
        ring = [sb("ring%d" % i, [128, 8, 128], BF16) for i in range(RING)]
        pn = fw.ps("pn")
        pss = fw.ps("pss")
        pacc = [fw.ps("pacc%d" % i) for i in range(6)]
        pi = [0]

        def pbank():
            p = pacc[pi[0] % len(pacc)]
            pi[0] += 1
            return p

        bigsub = [big.sub((slice(None), k, slice(None)), "bigsub%d" % k) for k in range(KC)]
        scr_i = {"yz": [0, bigsub[0:4]], "ot": [0, bigsub[4:8]], "sg": [0, bigsub[8:12]], "t1": [0, bigsub[12:14]]}

        def scr(name):
            ent = scr_i[name]
            t = ent[1][ent[0] % len(ent[1])]
            ent[0] += 1
            return t

        tmp_i = {}

        def tmp(name, shape=(128, NT), n=2, dtype=F32):
            if name not in tmp_i:
                tmp_i[name] = [0, [sb("%s_%d" % (name, i), shape, dtype) for i in range(n)]]
            ent = tmp_i[name]
            t = ent[1][ent[0] % n]
            ent[0] += 1
            return t

        wstate = {"issued": 0, "next": 0, "total": NWT * NTT}

        def get_w():
            idx = wstate["next"]
            while wstate["issued"] < min(idx + RING, wstate["total"]):
                k = wstate["issued"]
                fw.dma("pool", ring[k % RING][:], ws_d[k % NWT])
                wstate["issued"] += 1
            wstate["next"] += 1
            return ring[idx % RING]

        def proj(out_ps, rhs_of_kc, nk):
            w = None
            for kc in range(nk):
                if kc % 8 == 0:
                    w = get_w()
                mm(out_ps, w[:, kc % 8, :], rhs_of_kc(kc), start=(kc == 0), stop=(kc == nk - 1))

        def rms_rstd(blocks, nfeat, dst, f32=False):
            n = len(blocks)
            for i, v in enumerate(blocks):
                if f32:
                    sq = tmp("sqf")
                    act(sq[:], v, AF.Square)
                    mm(pn[:, 0:NT], ones, sq[:], start=(i == 0), stop=(i == n - 1))
                else:
                    sq = tmp("sqb", dtype=BF16)
                    act(sq[:], v, AF.Square)
                    mm(pn[:, 0:NT], ones_bf[:], sq[:], start=(i == 0), stop=(i == n - 1))
            lt = tmp("lnt", n=1)
            act(lt[:], pn[:, 0:NT], AF.Ln, bias=EPS, scale=1.0 / nfeat)
            act(dst[:], lt[:], AF.Exp, scale=-0.5)

        def norm_to_hT(gi):
            rs = tmp("rstd", n=1)
            rms_rstd([xs[:, kc, :] for kc in range(KC)], D, rs)
            for kc in range(KC):
                stt(hT[:, kc, :], xs[:, kc, :], gains[:, gi, kc:kc + 1], rs[:], ALU.mult, ALU.mult)

        def residual_add(gi):
            rs = tmp("rstd", n=1)
            rms_rstd([big[:, kc, :] for kc in range(KC)], D, rs)
            for kc in range(KC):
                t1 = tmp("resid", n=1)
                stt(t1[:], big[:, kc, :], gains[:, gi, kc:kc + 1], rs[:], ALU.mult, ALU.mult)
                tt(xs[:, kc, :], xs[:, kc, :], t1[:], ALU.add)

        for ti in range(NTT):
            t0 = ti * NT
            tsl = slice(t0, t0 + NT)
            fw.dma("sp", xs[:], xT_d[:, :, tsl])
            norm_to_hT(G_PRE)
            if ti > 0:
                fw.alias_barrier(yb, [upT])
                fw.alias_barrier(bigsub, [big])
            for blk in range(8):
                ot = scr("ot")
                fw.dma("sp", ot[:], odn_d[:, blk, tsl])
                sq = tmp("sqf")
                act(sq[:], ot[:], AF.Square)
                mm(pss[:, 0:NT], ones, sq[:])
                lt = tmp("lnt2", n=1)
                act(lt[:], pss[:, 0:NT], AF.Ln, bias=EPS, scale=1.0 / 128)
                rs = tmp("rs2", n=1)
                act(rs[:], lt[:], AF.Exp, scale=-0.5)
                pg = pbank()
                proj(pg[:, 0:NT], lambda kc: hT[:, kc, :], KC)
                sg = scr("sg")
                act(sg[:], pg[:, 0:NT], AF.Silu)
                t1 = scr("t1")
                stt(t1[:], ot[:], bn[:, 0:1], rs[:], ALU.mult, ALU.mult)
                tt(yb[0][:, blk, :], t1[:], sg[:], ALU.mult)
            for grp in range(2):
                yz = []
                for j in range(4):
                    blk = 4 * grp + j
                    ot = scr("ot")
                    fw.dma("sp", ot[:], ossm_d[:, blk, tsl])
                    pg = pbank()
                    proj(pg[:, 0:NT], lambda kc: hT[:, kc, :], KC)
                    sg = scr("sg")
                    act(sg[:], pg[:, 0:NT], AF.Silu)
                    y = scr("yz")
                    tt(y[:], ot[:], sg[:], ALU.mult)
                    sq = tmp("sqf")
                    act(sq[:], y[:], AF.Square)
                    mm(pss[:, 0:NT], ones, sq[:], start=(j == 0), stop=(j == 3))
                    yz.append(y)
                lt = tmp("lnt2", n=1)
                act(lt[:], pss[:, 0:NT], AF.Ln, bias=EPS, scale=1.0 / 512)
                rs = tmp("rs2", n=1)
                act(rs[:], lt[:], AF.Exp, scale=-0.5)
                for j in range(4):
                    blk = 4 * grp + j
                    stt(yb[1][:, blk, :], yz[j][:], bn[:, 1 + blk:2 + blk], rs[:], ALU.mult, ALU.mult)
            for hd in range(4):
                ots, sgs = [], []
                for j in range(2):
                    blk = 2 * hd + j
                    ot = scr("ot")
                    fw.dma("sp", ot[:], ogla_d[:, blk, tsl])
                    sq = tmp("sqf")
                    act(sq[:], ot[:], AF.Square)
                    mm(pss[:, 0:NT], ones, sq[:], start=(j == 0), stop=(j == 1))
                    pg = pbank()
                    proj(pg[:, 0:NT], lambda kc: hT[:, kc, :], KC)
                    sg = scr("sg")
                    act(sg[:], pg[:, 0:NT], AF.Silu)
                    ots.append(ot)
                    sgs.append(sg)
                lt = tmp("lnt2", n=1)
                act(lt[:], pss[:, 0:NT], AF.Ln, bias=EPS, scale=1.0 / 256)
                rs = tmp("rs2", n=1)
                act(rs[:], lt[:], AF.Exp, scale=-0.5)
                for j in range(2):
                    blk = 2 * hd + j
                    t1 = scr("t1")
                    stt(t1[:], ots[j][:], bn[:, 9 + j:10 + j], rs[:], ALU.mult, ALU.mult)
                    tt(yb[2][:, blk, :], t1[:], sgs[j][:], ALU.mult)
            for dblk in range(KC):
                macc = tmp("macc")
                for b in range(3):
                    pg = pbank()
                    proj(pg[:, 0:NT], lambda kc: hT[:, kc, :], KC)
                    sgm = tmp("sgm", n=2)
                    act(sgm[:], pg[:, 0:NT], AF.Sigmoid)
                    pu = pbank()
                    proj(pu[:, 0:NT], lambda kc, b=b: yb[b][:, kc, :], 8)
                    if b == 0:
                        tt(macc[:], pu[:, 0:NT], sgm[:], ALU.mult)
                    else:
                        t2 = tmp("t2", n=1)
                        tt(t2[:], pu[:, 0:NT], sgm[:], ALU.mult)
                        if b == 1:
                            tt(macc[:], macc[:], t2[:], ALU.add)
                        else:
                            tt(mp[:, dblk, :], macc[:], t2[:], ALU.add)
            fw.alias_barrier([big], bigsub)
            for dblk in range(KC):
                po = pbank()
                proj(po[:, 0:NT], lambda kc: mp[:, kc, :], KC)
                cp(big[:, dblk, :], po[:, 0:NT])
            residual_add(G_POSTMIX)
            norm_to_hT(G_PREMLP)
            fw.alias_barrier([upT], yb)
            for half in range(2):
                for fb in range(32):
                    pu = pbank()
                    proj(pu[:, 0:NT], lambda kc: hT[:, kc, :], KC)
                    r = tmp("relu")
                    act(r[:], pu[:, 0:NT], AF.Relu)
                    tt(upT[:, fb, :], r[:], r[:], ALU.mult)
                for dblk in range(KC):
                    pd = pbank()
                    proj(pd[:, 0:NT], lambda kc: upT[:, kc, :], 32)
                    if half == 0:
                        cp(big[:, dblk, :], pd[:, 0:NT])
                    else:
                        tt(big[:, dblk, :], pd[:, 0:NT], big[:, dblk, :], ALU.add)
            residual_add(G_POSTMLP)
            norm_to_hT(G_PLEPRE)
            pf = tmp("pf", (128, 2, NT), n=1)
            fw.dma("sp", pf[:], pT_d[:, :, tsl])
            pb = tmp("pb", (128, 2, NT), n=1, dtype=BF16)
            cp(pb[:], pf[:], e="dve")
            for dblk in range(KC):
                pg = pbank()
                proj(pg[:, 0:NT], lambda kc: hT[:, kc, :], KC)
                sgm = tmp("sgm", n=2)
                act(sgm[:], pg[:, 0:NT], AF.Sigmoid)
                pe_ = pbank()
                for c in range(2):
                    mm(pe_[:, 0:NT], wpp[:, c, dblk * 128:dblk * 128 + 128], pb[:, c, :], start=(c == 0), stop=(c == 1))
                tt(big[:, dblk, :], pe_[:, 0:NT], sgm[:], ALU.mult)
            residual_add(G_PLEPOST)
            fw.dma("sp", out_d[:, :, tsl], xs[:])
        assert wstate["next"] == wstate["total"], (wstate, NWT)
        fw.finish()
        print("dense sbuf bytes/partition", fw.sb_bytes + 32 * NT * 2, "instr", fw.cnt)
    return nc


def half_tile(W, r0, c0):
    return W[r0:r0 + 1024, c0:c0 + 128].reshape(8, 128, 128).transpose(1, 0, 2)


def build_wstream(P, L):
    w_in, wb, wo = P["w_in"][L], P["w_branch"][L], P["w_out"][L]
    wu, wd, wg = P["w_up"][L], P["w_down"][L], P["w_ple_gate"][L]
    ws = np.empty((NWT, 128, 8, 128), np.float32)
    i = 0

    def put(W, r0, c0):
        nonlocal i
        ws[i] = half_tile(W, r0, c0)
        i += 1

    for off in (O_DNG, O_SZ, O_GO):
        for blk in range(8):
            put(w_in, 0, off + blk * 128)
            put(w_in, 1024, off + blk * 128)
    for dblk in range(16):
        for b in range(3):
            put(w_in, 0, O_BR + b * D + dblk * 128)
            put(w_in, 1024, O_BR + b * D + dblk * 128)
            put(wb[b], 0, dblk * 128)
    for dblk in range(16):
        put(wo, 0, dblk * 128)
        put(wo, 1024, dblk * 128)
    for half in range(2):
        for fb in range(32):
            put(wu, 0, (half * 32 + fb) * 128)
            put(wu, 1024, (half * 32 + fb) * 128)
        for dblk in range(16):
            for q in range(4):
                put(wd, half * 4096 + q * 1024, dblk * 128)
    for dblk in range(16):
        put(wg, 0, dblk * 128)
        put(wg, 1024, dblk * 128)
    assert i == NWT, i
    return ws


def tok_to_fm(a):
    Tn, C = a.shape
    return np.ascontiguousarray(a.reshape(Tn, C // 128, 128).transpose(2, 1, 0))


def fm_to_tok(a):
    p, nb, Tn = a.shape
    return np.ascontiguousarray(a.transpose(2, 1, 0).reshape(Tn, nb * 128))


def dense_shared_inputs(L, P, consts):
    gains = np.stack([P[k][L].reshape(KC, 128).T for k in
                      ("pre_mix_norm", "post_mix_norm", "pre_mlp_norm", "post_mlp_norm", "ple_pre_norm", "ple_post_norm")], axis=1)
    bn = np.concatenate([P["dn_norm"][L].reshape(128, 1), P["ssm_norm"][L].reshape(8, 128).T,
                         P["gla_norm"][L].reshape(2, 128).T], axis=1)
    return {
        "wstream": build_wstream(P, L),
        "wpp": fm_layout(P["w_ple_proj"][L]),
        "gains": np.ascontiguousarray(gains, dtype=np.float32),
        "bnorm": np.ascontiguousarray(bn, dtype=np.float32),
        "consts": consts,
    }


SEQ = 8192
NCORE = 8
TCORE = SEQ // NCORE
_PROG = {}


def _prog(kind):
    if kind not in _PROG:
        _PROG[kind] = build_mixer(SEQ) if kind == "mixer" else build_dense(TCORE)
    return _PROG[kind]


def kernel(**inputs):
    P = {k: np.asarray(v) for k, v in inputs.items()}
    x = np.ascontiguousarray(P["x"][0], dtype=np.float32)
    consts = make_consts()
    cores = list(range(NCORE))
    for L in range(2):
        xT_l = tok_to_fm(x)
        in_maps = [mixer_core_inputs(c, L, xT_l, P, consts) for c in cores]
        res = run_bass_kernel_spmd(_prog("mixer"), in_maps, core_ids=cores).results
        o_dn = np.concatenate([res[c]["o_dn"] for c in cores], axis=1)
        o_ssm = np.concatenate([res[c]["o_ssm"] for c in cores], axis=1)
        o_gla = np.concatenate([res[c]["o_gla"] for c in cores], axis=1)
        del res, in_maps
        sh = dense_shared_inputs(L, P, consts)
        in_maps = []
        for c in cores:
            ts_ = slice(c * TCORE, (c + 1) * TCORE)
            im = dict(sh)
            im.update({"xT": tok_to_fm(x[ts_]), "odnT": tok_to_fm(o_dn[ts_]), "ossmT": tok_to_fm(o_ssm[ts_]),
                       "oglaT": tok_to_fm(o_gla[ts_]), "pT": tok_to_fm(np.ascontiguousarray(P["p"][L][0, ts_]))})
            in_maps.append(im)
        res = run_bass_kernel_spmd(_prog("dense"), in_maps, core_ids=cores).results
        x = np.concatenate([fm_to_tok(res[c]["xoT"]) for c in cores], axis=0)
        del res, in_maps, sh
    return x[None].astype(np.float32)
```

```python
import numpy as np
from contextlib import ExitStack
import concourse.bass as bass
import concourse.mybir as mybir
from concourse.bass_utils import run_bass_kernel_spmd

F32 = mybir.dt.float32
BF16 = mybir.dt.bfloat16
F32R = mybir.dt.float32r
USE_F32R = True
ALU = mybir.AluOpType
AF = mybir.ActivationFunctionType

D = 2048
KC = 16
EPS = 1e-6
NEG = -30000.0
PE_ = "dve"


class Trk:
    __slots__ = ("w", "r", "dsem", "dcnt", "name", "excl")

    def __init__(self, name):
        self.excl = False
        self.w = None
        self.r = {}
        self.dsem = None
        self.dcnt = 0
        self.name = name


class V:
    __slots__ = ("ap", "trk")

    def __init__(self, ap, trk):
        self.ap = ap
        self.trk = trk

    def __getitem__(self, idx):
        return V(self.ap[idx], self.trk)


class TT:
    def __init__(self, ap, name):
        self.ap = ap
        self.trk = Trk(name)

    def __getitem__(self, idx):
        return V(self.ap[idx], self.trk)

    def view(self, idx):
        t = TT(self.ap[idx], self.trk.name)
        t.trk = self.trk
        return t

    def sub(self, idx, name=None):
        return TT(self.ap[idx], name or self.trk.name + "_sub")


class FW:
    def __init__(self, nc, es):
        self.nc = nc
        self.es = es
        self.eng = {"pe": nc.tensor, "dve": nc.vector, "act": nc.scalar, "pool": nc.gpsimd, "sp": nc.sync}
        self.sem = {}
        self.cnt = {}
        self.waited = {}
        for e in self.eng:
            self.sem[e] = es.enter_context(nc.semaphore("sem_" + e))
            self.cnt[e] = 0
            self.waited[e] = {}
        self.semowner = {id(self.sem[e]): e for e in self.eng}
        self.nsem = len(self.eng)
        self.out_dmas = []
        self.uid = 0

    def sb(self, name, shape, dtype=F32):
        n = 1
        for d in shape[1:]:
            n *= d
        self.sb_bytes = getattr(self, "sb_bytes", 0) + n * (2 if dtype == BF16 else 4)
        return TT(self.es.enter_context(self.nc.sbuf_tensor("s_" + name, list(shape), dtype)), name)

    def ps(self, name, shape=(128, 512), dtype=F32):
        t = TT(self.es.enter_context(self.nc.psum_tensor("p_" + name, list(shape), dtype)), name)
        t.trk.excl = True
        return t

    def _wait(self, e, dep):
        sem, val = dep
        k = id(sem)
        if self.semowner.get(k) == e and e == "pe":
            return
        if self.waited[e].get(k, 0) >= val:
            return
        self.eng[e].wait_ge(sem, val)
        self.waited[e][k] = val

    def _sync(self, e, reads, writes):
        for t in reads:
            if t.w is not None:
                self._wait(e, t.w)
        for t in writes:
            if t.w is not None:
                self._wait(e, t.w)
            for k, dep in t.r.items():
                self._wait(e, dep)

    def op(self, e, name, **kw):
        reads, writes = [], []
        args = {}
        for k, v in kw.items():
            if isinstance(v, V):
                (writes if (k in ("out", "accum_out") or v.trk.excl) else reads).append(v.trk)
                args[k] = v.ap
            else:
                args[k] = v
        self._sync(e, reads, writes)
        inst = getattr(self.eng[e], name)(**args)
        self.cnt[e] += 1
        inst.then_inc(self.sem[e], 1)
        me = (self.sem[e], self.cnt[e])
        for t in reads:
            t.r[id(self.sem[e])] = me
        for t in writes:
            t.w = me
            t.r = {}
        return inst

    def dma(self, e, out, in_, **kw):
        if isinstance(out, V):
            t = out.trk
            self._sync(e, [], [t])
            if t.dsem is None:
                t.dsem = self.es.enter_context(self.nc.semaphore("dsem%d" % self.nsem))
                self.nsem += 1
            self.eng[e].dma_start(out=out.ap, in_=in_, **kw).then_inc(t.dsem, 16)
            t.dcnt += 16
            t.w = (t.dsem, t.dcnt)
            t.r = {}
        else:
            t = in_.trk
            self._sync(e, [t], [])
            if t.dsem is None:
                t.dsem = self.es.enter_context(self.nc.semaphore("dsem%d" % self.nsem))
                self.nsem += 1
            self.eng[e].dma_start(out=out, in_=in_.ap, **kw).then_inc(t.dsem, 16)
            t.dcnt += 16
            t.r[id(t.dsem)] = (t.dsem, t.dcnt)
            self.out_dmas.append((t.dsem, t.dcnt))

    def alias_barrier(self, new, old):
        deps = {}
        for t in old:
            for dep in ([t.trk.w] if t.trk.w is not None else []) + list(t.trk.r.values()):
                k = id(dep[0])
                if k not in deps or deps[k][1] < dep[1]:
                    deps[k] = dep
        for t in new:
            for k, dep in deps.items():
                if k not in t.trk.r or t.trk.r[k][1] < dep[1]:
                    t.trk.r[k] = dep

    def finish(self):
        last = {}
        for sem, val in self.out_dmas:
            last[id(sem)] = (sem, val)
        for sem, val in last.values():
            self._wait("sp", (sem, val))

    def mm(self, out, lhsT, rhs, start=True, stop=True):
        return self.op("pe", "matmul", out=out, lhsT=lhsT, rhs=rhs, start=start, stop=stop)

    @staticmethod
    def r(v):
        return V(v.ap.bitcast(F32R), v.trk) if USE_F32R else v

    def mmr(self, out, lhsT, rhs, start=True, stop=True):
        if USE_F32R:
            lhsT = V(lhsT.ap.bitcast(F32R), lhsT.trk)
            rhs = V(rhs.ap.bitcast(F32R), rhs.trk)
        return self.op("pe", "matmul", out=out, lhsT=lhsT, rhs=rhs, start=start, stop=stop)

    def tr(self, out, in_, ident):
        return self.op("pe", "transpose", out=out, in_=in_, identity=ident)

    def act(self, out, in_, func, bias=None, scale=None, e="act"):
        kw = {}
        if bias is not None:
            kw["bias"] = bias
        if scale is not None:
            kw["scale"] = scale
        return self.op(e, "activation", out=out, in_=in_, func=func, **kw)

    def ts(self, out, in0, s1, op0, s2=None, op1=None, e="dve"):
        if op1 is None:
            return self.op(e, "tensor_scalar", out=out, in0=in0, scalar1=s1, scalar2=None, op0=op0)
        return self.op(e, "tensor_scalar", out=out, in0=in0, scalar1=s1, scalar2=s2, op0=op0, op1=op1)

    def stt(self, out, in0, scalar, in1, op0, op1):
        return self.op("dve", "scalar_tensor_tensor", out=out, in0=in0, scalar=scalar, in1=in1, op0=op0, op1=op1)

    def tt(self, out, in0, in1, op, e="dve"):
        return self.op(e, "tensor_tensor", out=out, in0=in0, in1=in1, op=op)

    def cp(self, out, in_, e="act"):
        if e == "act":
            return self.op("act", "copy", out=out, in_=in_)
        return self.op(e, "tensor_copy", out=out, in_=in_)


C_ID, C_ONES, C_TRI, C_BLK, C_NMT, C_PMS, C_M01, C_RST = range(8)
NCONST = 8


def make_consts():
    p = np.arange(128)[:, None]
    f = np.arange(128)[None, :]
    same = (p // 64) == (f // 64)
    c = np.zeros((128, NCONST, 128), np.float32)
    c[:, C_ID] = (p == f)
    c[:, C_ONES] = 1.0
    c[:, C_TRI] = (same & (p <= f))
    c[:, C_BLK] = same
    c[:, C_NMT] = np.where(same & (f >= p), 0.0, NEG)
    c[:, C_PMS] = np.where(same & (p > f), 0.0, -NEG)
    c[:, C_M01] = (same & (f >= p))
    c[:, C_RST] = ((f % 64) != 0)
    return c


NFM = 8 * 128 + 16
NTM = 132
S_DN_ALOG, S_DN_DTB, S_S_DTB0, S_S_DTB1, S_S_ALOG0, S_S_ALOG1, S_S_D0, S_S_D1 = range(8)


def build_mixer(T, ST=256, parts=("dn", "ssm", "gla"), dn_stage=99):
    nc = bass.Bass("TRN2", target_bir_lowering=False)
    NST = T // ST
    NSUB = ST // 128

    def din(name, shape):
        return nc.dram_tensor(name, list(shape), F32, kind="ExternalInput").ap()

    xT_d = din("xT", [128, KC, T])
    gain_d = din("gain", [128, KC])
    wfm_d = din("wfm", [128, KC, NFM])
    wtm_d = din("wtm", [128, KC, NTM])
    cw_d = din("cw", [128, 6, 4])
    cb_d = din("cb", [128, 3])
    scal_d = din("scal", [128, 8])
    w2_d = din("w2", [16, 128])
    gb_d = din("gb", [128, 1])
    const_d = din("consts", [128, NCONST, 128])
    odn_d = nc.dram_tensor("o_dn", [T, 128], F32, kind="ExternalOutput").ap()
    ossm_d = nc.dram_tensor("o_ssm", [T, 128], F32, kind="ExternalOutput").ap()
    ogla_d = nc.dram_tensor("o_gla", [T, 128], F32, kind="ExternalOutput").ap()

    with ExitStack() as es:
        fw = FW(nc, es)
        sb, mm, tr, act, ts, stt, tt, cp = fw.sb, fw.mm, fw.tr, fw.act, fw.ts, fw.stt, fw.tt, fw.cp
        mmr = fw.mmr
        r_ = fw.r

        wfm = sb("wfm", [128, KC, NFM], BF16)
        wtm = sb("wtm", [128, KC, NTM], BF16)
        gain = sb("gain", [128, KC])
        cw = sb("cw", [128, 6, 4])
        cb = sb("cb", [128, 3])
        scal = sb("scal", [128, 8])
        w2 = sb("w2", [16, 128])
        gb = sb("gb", [128, 1])
        consts = sb("consts", [128, NCONST, 128])
        fw.dma("sp", consts[:], const_d)
        fw.dma("sp", gain[:], gain_d)
        fw.dma("sp", cw[:], cw_d)
        fw.dma("sp", cb[:], cb_d)
        fw.dma("sp", scal[:], scal_d)
        fw.dma("sp", w2[:], w2_d)
        fw.dma("sp", gb[:], gb_d)
        wparts = []
        for g in range(4):
            wp = wfm.sub((slice(None), slice(g * 4, g * 4 + 4), slice(None)), "wfm%d" % g)
            fw.dma("pool", wp[:], wfm_d[:, g * 4:g * 4 + 4, :])
            wparts.append(wp)
        fw.dma("pool", wtm[:], wtm_d)

        def wf(kc, c0, c1):
            return wparts[kc // 4][:, kc % 4, c0:c1]

        def cst(i):
            return consts[:, i, :]

        ident, ones, tri, blk = cst(C_ID), cst(C_ONES), cst(C_TRI), cst(C_BLK)
        nmt, pms, m01, rst = cst(C_NMT), cst(C_PMS), cst(C_M01), cst(C_RST)

        ones_bf = sb("ones_bf", [128, 128], BF16)
        cp(ones_bf[:], ones, e="dve")
        nega = sb("nega", [128, 4])
        act(nega[:, 0:1], scal[:, S_DN_ALOG:S_DN_ALOG + 1], AF.Exp)
        act(nega[:, 1:3], scal[:, S_S_ALOG0:S_S_ALOG1 + 1], AF.Exp)
        ts(nega[:, 0:3], nega[:, 0:3], -1.0, ALU.mult)
        ngb = sb("ngb", [128, 1])
        ts(ngb[:], gb[:], -1.0, ALU.mult)

        xt = [sb("xt%d" % i, [128, KC, ST]) for i in range(2)]
        sqb = [sb("sqb%d" % i, [128, ST], BF16) for i in range(2)]
        hT = [sb("hT%d" % i, [128, KC, ST], BF16) for i in range(2)]
        lnt = sb("lnt", [128, ST])
        rstd = sb("rstd", [128, ST])
        raw = [sb("raw%d" % b, [128, ST + 3]) for b in range(6)]
        cva = [sb("cva%d" % b, [128, ST]) for b in range(2)]
        cvq = [sb("cvq%d" % b, [128, ST]) for b in range(2)]
        cvo = [[sb("cvo%d_%d" % (i, b), [128, ST]) for b in range(4)] for i in range(2)]
        gq = [sb("gq%d" % i, [128, ST]) for i in range(2)]
        gk_ = [sb("gk%d" % i, [128, ST]) for i in range(2)]
        lrT = [sb("lrT%d" % i, [16, ST]) for i in range(2)]
        sqf = sb("sqf", [128, ST])
        qn = [sb("qn%d" % i, [128, ST]) for i in range(2)]
        kn = [sb("kn%d" % i, [128, ST]) for i in range(2)]
        tms = [[sb("tm%d_%d" % (i, s), [128, 128]) for s in range(NSUB)] for i in range(2)]
        tmc = [[sb("tmc%d_%d" % (i, s), [128, 4]) for s in range(NSUB)] for i in range(2)]
        for b in range(6):
            ts(raw[b][:, 0:3], consts[:, C_ONES, 0:3], 0.0, ALU.mult)

        pnb = fw.ps("pnb")
        pacc = pnb
        pn = pnb.view((slice(None), slice(0, 256)))
        ptm = pnb.view((slice(None), slice(256, 256 + NTM)))
        pools = {"dn": [fw.ps("pdn%d" % i) for i in range(2)], "dns": [fw.ps("pds%d" % i) for i in range(2)],
                 "ssm": [fw.ps("pss%d" % i) for i in range(2)], "gla": [fw.ps("pgl%d" % i) for i in range(1)]}
        pool_i = {"dn": 0, "dns": 0, "ssm": 0, "gla": 0}

        def pslot(which):
            lst = pools[which]
            b = lst[pool_i[which] % len(lst)]
            pool_i[which] += 1
            return b.view((slice(None), slice(0, 128)))

        tmp_i = {}

        def tmp(name, shape=(128, 128), n=2, dtype=F32):
            if name not in tmp_i:
                tmp_i[name] = [0, [sb("%s_%d" % (name, i), shape, dtype) for i in range(n)]]
            ent = tmp_i[name]
            t = ent[1][ent[0] % n]
            ent[0] += 1
            return t

        S_dn = sb("S_dn", [128, 128])
        S_ssm = sb("S_ssm", [128, 128])
        S_gla = sb("S_gla", [128, 128])
        for S in (S_dn, S_ssm, S_gla):
            ts(r_(S[:]), ones, 0.0, ALU.mult)

        KS = 128.0 ** -0.5

        def prologue(st):
            par = st % 2
            t0 = st * ST
            x = xt[par]
            h = hT[par]
            fw.dma("sp", x[:], xT_d[:, :, t0:t0 + ST])
            for kc in range(KC):
                sq = sqb[kc % 2]
                act(sq[:], x[:, kc, :], AF.Square)
                mm(pn[:, 0:ST], ones_bf[:], sq[:], start=(kc == 0), stop=(kc == KC - 1))
            yield
            act(lnt[:], pn[:, 0:ST], AF.Ln, bias=EPS, scale=1.0 / D)
            act(rstd[:], lnt[:], AF.Exp, scale=-0.5)
            for kc in range(KC):
                stt(h[:, kc, :], x[:, kc, :], gain[:, kc:kc + 1], rstd[:], ALU.mult, ALU.mult)
                if kc % 4 == 3:
                    yield
            for b in range(9):
                M = 128 if b < 8 else 16
                for kc in range(KC):
                    mm(pacc[0:M, 0:ST], wf(kc, b * 128, b * 128 + M), h[:, kc, :], start=(kc == 0), stop=(kc == KC - 1))
                if b < 6:
                    cp(raw[b][:, 3:3 + ST], pacc[:, 0:ST])
                elif b == 6:
                    cp(gq[par][:], pacc[:, 0:ST])
                elif b == 7:
                    cp(gk_[par][:], pacc[:, 0:ST])
                else:
                    cp(lrT[par][:], pacc[0:16, 0:ST])
                yield
            for b in range(6):
                ca = cva[b % 2]
                if b >= 3:
                    ts(ca[:], raw[b][:, 0:ST], cw[:, b, 0:1], ALU.mult, cb[:, b - 3:b - 2], ALU.add)
                else:
                    ts(ca[:], raw[b][:, 0:ST], cw[:, b, 0:1], ALU.mult)
                for k in range(1, 4):
                    stt(ca[:], raw[b][:, k:k + ST], cw[:, b, k:k + 1], ca[:], ALU.mult, ALU.add)
                dst = cvq[b] if b < 2 else cvo[par][b - 2]
                act(r_(dst[:]) if b >= 4 else dst[:], ca[:], AF.Silu)
                cp(raw[b][:, 0:3], raw[b][:, ST:ST + 3], e="dve")
                yield
            for src, dst in ((cvq[0], qn[par]), (cvq[1], kn[par])):
                act(sqf[:], src[:], AF.Square)
                mm(pn[:, 0:ST], ones, sqf[:])
                act(lnt[:], pn[:, 0:ST], AF.Ln, bias=EPS)
                act(lnt[:], lnt[:], AF.Exp, scale=-0.5)
                tt(r_(dst[:]), src[:], lnt[:], ALU.mult)
                yield
            for s in range(NSUB):
                cs = slice(s * 128, s * 128 + 128)
                for kc in range(KC):
                    mm(ptm[:, 0:NTM], h[:, kc, cs], wtm[:, kc, :], start=(kc == 0), stop=(kc == KC - 1))
                cp(r_(tms[par][s][:]), ptm[:, 0:128])
                cp(tmc[par][s][:], ptm[:, 128:NTM])
                yield

        dn_store = {}

        def dn_prep(st, s):
            par = st % 2
            cs = slice(s * 128, s * 128 + 128)
            tm = tms[par][s]
            qn_, kn_, vT = qn[par], kn[par], cvo[par][0]
            P = lambda: pslot("dn")
            sm = tmp("dn_sm", (128, 16))
            act(sm[:, 0:1], tmc[par][s][:, 0:1], AF.Sigmoid)
            act(sm[:, 1:2], tmc[par][s][:, 1:2], AF.Exp, bias=scal[:, S_DN_DTB:S_DN_DTB + 1])
            act(sm[:, 2:3], sm[:, 1:2], AF.Ln, bias=1.0)
            ts(sm[:, 3:4], sm[:, 2:3], nega[:, 0:1], ALU.mult)
            ts(sm[:, 4:5], sm[:, 0:1], -1.0, ALU.mult)
            Gbc = tmp("Gbc")
            ts(Gbc[:], ones, sm[:, 3:4], ALU.mult)
            p_gc = P()
            mm(p_gc[:], Gbc[:], tri)
            p_c = P()
            mm(p_c[:, 0:1], tri, sm[:, 3:4])
            mm(p_c[:, 1:2], blk, sm[:, 3:4])
            yield
            cp(sm[:, 5:7], p_c[:, 0:2], e="dve")
            ET = tmp("ET")
            stt(ET[:], p_gc[:], sm[:, 5:6], nmt, ALU.subtract, ALU.add)
            decT = tmp("decT")
            act(decT[:], ET[:], AF.Exp)
            ES = tmp("ES")
            stt(ES[:], p_gc[:], sm[:, 5:6], pms, ALU.subtract, ALU.add)
            decS = tmp("decS")
            act(decS[:], ES[:], AF.Exp, scale=-1.0)
            egbc = tmp("egbc")
            act(egbc[:], p_gc[:], AF.Exp)
            act(sm[:, 7:8], sm[:, 5:6], AF.Exp)
            tt(sm[:, 10:11], sm[:, 6:7], sm[:, 5:6], ALU.subtract)
            act(sm[:, 8:9], sm[:, 10:11], AF.Exp)
            tt(sm[:, 9:10], sm[:, 0:1], sm[:, 7:8], ALU.mult)
            p_kk = P()
            mmr(p_kk[:], kn_[:, cs], kn_[:, cs])
            p_qk = P()
            mmr(p_qk[:], kn_[:, cs], qn_[:, cs])
            yield
            Nm = tmp("Nm", n=3)
            stt(r_(Nm[:]), p_kk[:], sm[:, 4:5], decS[:], ALU.mult, ALU.mult)
            attnT = tmp("attnT")
            stt(r_(attnT[:]), p_qk[:], KS, decT[:], ALU.mult, ALU.mult)
            p_b = P()
            tr(p_b[:], Nm[:], ident)
            yield
            Bm = tmp("Bm", n=3)
            cp(r_(Bm[:]), p_b[:])
            R = tmp("R", n=3)
            tt(r_(R[:]), p_b[:], ident, ALU.add)
            curN, curB = Nm, Bm
            for step in range(5):
                p_n2 = P()
                mmr(p_n2[:], curB[:], curN[:])
                if step < 4:
                    p_b2 = P()
                    mmr(p_b2[:], curN[:], curB[:])
                yield
                N2 = tmp("Nm", n=3)
                cp(r_(N2[:]), p_n2[:])
                if step < 4:
                    B2 = tmp("Bm", n=3)
                    cp(r_(B2[:]), p_b2[:], e="dve")
                p_r = P()
                mmr(p_r[:], N2[:], R[:])
                yield
                R2 = tmp("R", n=3)
                tt(r_(R2[:]), p_r[:], R[:], ALU.add)
                R = R2
                curN = N2
                if step < 4:
                    curB = B2
            p_kt = P()
            tr(p_kt[:], kn_[:, cs], ident)
            p_vt = P()
            tr(p_vt[:], vT[:, cs], ident)
            yield
            kbg = tmp("kbg")
            ts(r_(kbg[:]), p_kt[:], sm[:, 9:10], ALU.mult)
            kd = tmp("kd")
            ts(r_(kd[:]), p_kt[:], sm[:, 8:9], ALU.mult)
            bv = tmp("bv")
            ts(r_(bv[:]), p_vt[:], sm[:, 0:1], ALU.mult)
            p_w = P()
            mmr(p_w[:], kbg[:], R[:])
            p_u = P()
            mmr(p_u[:], R[:], bv[:])
            yield
            wT = tmp("wT")
            cp(r_(wT[:]), p_w[:])
            u = tmp("u")
            cp(r_(u[:]), p_u[:])
            qgT = tmp("qgT")
            stt(r_(qgT[:]), qn_[:, cs], KS, egbc[:], ALU.mult, ALU.mult)
            dn_store[(st, s)] = (wT, u, qgT, attnT, kd, egbc)

        def dn_scan(st, s):
            tg = st * ST + s * 128
            P = lambda: pslot("dns")
            wT, u, qgT, attnT, kd, egbc = dn_store.pop((st, s))
            o_dn = tmp("o_dn")
            vnew = tmp("vnew")
            for c in range(2):
                r = slice(64 * c, 64 * c + 64)
                p1 = P()
                mmr(p1[:], wT[:], S_dn[:])
                yield
                tt(r_(vnew[r, :]), u[r, :], p1[r, :], ALU.subtract)
                po = P()
                mmr(po[:], qgT[:], S_dn[:], start=True, stop=False)
                mmr(po[:], attnT[r, :], vnew[r, :], start=False, stop=True)
                p_s = P()
                mmr(p_s[:], kd[r, :], vnew[r, :])
                yield
                cp(o_dn[r, :], po[r, :])
                gl = egbc[:, 64 * c + 63:64 * c + 64]
                stt(r_(S_dn[:]), S_dn[:], gl, p_s[:], ALU.mult, ALU.add)
            fw.dma("sp", odn_d[tg:tg + 128, :], o_dn[:])

        def ssm_tile(st, s):
            par = st % 2
            cs = slice(s * 128, s * 128 + 128)
            tg = st * ST + s * 128
            tm = tms[par][s]
            P = lambda: pslot("ssm")
            ss = tmp("ss_sm", (128, 16))
            act(ss[:, 0:1], tmc[par][s][:, 2:3], AF.Exp, bias=scal[:, S_S_DTB0:S_S_DTB0 + 1])
            act(ss[:, 1:2], tmc[par][s][:, 3:4], AF.Exp, bias=scal[:, S_S_DTB1:S_S_DTB1 + 1])
            act(ss[:, 2:4], ss[:, 0:2], AF.Ln, bias=1.0)
            tt(ss[:, 4:6], ss[:, 2:4], nega[:, 1:3], ALU.mult)
            p_c = P()
            mm(p_c[:, 0:2], tri, ss[:, 4:6])
            mm(p_c[:, 2:4], blk, ss[:, 4:6])
            yield
            cp(ss[:, 6:10], p_c[:, 0:4], e="dve")
            tt(ss[:, 12:14], ss[:, 8:10], ss[:, 6:8], ALU.subtract)
            act(ss[:, 10:12], ss[:, 12:14], AF.Exp)
            xT_s, BT_s, CT_s = cvo[par][1][:, cs], cvo[par][2][:, cs], cvo[par][3][:, cs]
            LTs, eab = [], []
            for h in range(2):
                Abc = tmp("Abc")
                ts(Abc[:], ones, ss[:, 4 + h:5 + h], ALU.mult)
                p_a = P()
                mm(p_a[:], Abc[:], tri)
                yield
                ETs = tmp("ETs")
                stt(ETs[:], p_a[:], ss[:, 6 + h:7 + h], nmt, ALU.subtract, ALU.add)
                LT = tmp("LT", n=4)
                act(LT[:], ETs[:], AF.Exp)
                ea = tmp("eab", n=4)
                act(ea[:], p_a[:], AF.Exp)
                LTs.append(LT)
                eab.append(ea)
            p_cb = P()
            mmr(p_cb[:], BT_s, CT_s)
            yield
            WT, CgT = [], []
            for h in range(2):
                w_ = tmp("WT", n=4)
                tt(r_(w_[:]), p_cb[:], LTs[h][:], ALU.mult)
                cg = tmp("CgT", n=4)
                tt(r_(cg[:]), CT_s, eab[h][:], ALU.mult)
                WT.append(w_)
                CgT.append(cg)
            p_xt = P()
            tr(p_xt[:], xT_s, ident)
            yield
            xs = tmp("xs")
            cp(xs[:], p_xt[:])
            xdt = tmp("xdt")
            xdd = tmp("xdd")
            for h in range(2):
                hc = slice(64 * h, 64 * h + 64)
                ts(r_(xdt[:, hc]), p_xt[:, hc], ss[:, 2 + h:3 + h], ALU.mult)
                ts(r_(xdd[:, hc]), xdt[:, hc], ss[:, 10 + h:11 + h], ALU.mult)
            p_bt = P()
            tr(p_bt[:], BT_s, ident)
            yield
            Btok = tmp("Btok")
            cp(r_(Btok[:]), p_bt[:])
            y_ssm = tmp("y_ssm")
            for c in range(2):
                r = slice(64 * c, 64 * c + 64)
                po = P()
                for h in range(2):
                    hc = slice(64 * h, 64 * h + 64)
                    mmr(po[:, hc], CgT[h][:], S_ssm[:, hc], start=True, stop=False)
                    mmr(po[:, hc], WT[h][r, :], xdt[r, hc], start=False, stop=True)
                p_s = P()
                mmr(p_s[:], Btok[r, :], xdd[r, :])
                yield
                for h in range(2):
                    hc = slice(64 * h, 64 * h + 64)
                    stt(y_ssm[r, hc], xs[r, hc], scal[r, S_S_D0 + h:S_S_D0 + h + 1], po[r, hc], ALU.mult, ALU.add)
                for h in range(2):
                    hc = slice(64 * h, 64 * h + 64)
                    stt(r_(S_ssm[:, hc]), S_ssm[:, hc], eab[h][:, 64 * c + 63:64 * c + 64], p_s[:, hc], ALU.mult, ALU.add)
            fw.dma("sp", ossm_d[tg:tg + 128, :], y_ssm[:])

        def gla_tile(st, s):
            par = st % 2
            cs = slice(s * 128, s * 128 + 128)
            tg = st * ST + s * 128
            tm = tms[par][s]
            P = lambda: pslot("gla")
            p_g = P()
            mm(p_g[:], w2[:], lrT[par][:, cs])
            yield
            eg = tmp("g_e")
            act(eg[:], p_g[:], AF.Exp, bias=ngb[:], scale=-1.0)
            lT = tmp("g_l")
            act(lT[:], eg[:], AF.Ln, bias=1.0)
            LcT = tmp("g_Lc")
            fw.op("dve", "tensor_tensor_scan", out=LcT[:], data0=rst, data1=lT[:], initial=0.0,
                  op0=ALU.mult, op1=ALU.add)
            gs = tmp("g_sm", (128, 4))
            ts(gs[:, 0:1], LcT[:, 63:64], -1.0 / 16.0, ALU.mult)
            ts(gs[:, 1:2], LcT[:, 127:128], -1.0 / 16.0, ALU.mult)
            egT = tmp("g_eg")
            act(egT[:], LcT[:], AF.Exp, scale=-1.0 / 16.0)
            engT = tmp("g_eng")
            act(engT[:], LcT[:], AF.Exp, scale=1.0 / 16.0)
            edT = tmp("g_ed")
            for c in range(2):
                act(edT[:, 64 * c:64 * c + 64], LcT[:, 64 * c:64 * c + 64], AF.Exp, bias=gs[:, c:c + 1], scale=1.0 / 16.0)
            yield
            qpT = tmp("g_qp")
            stt(r_(qpT[:]), gq[par][:, cs], KS, egT[:], ALU.mult, ALU.mult)
            kppT = tmp("g_kpp")
            tt(r_(kppT[:]), gk_[par][:, cs], engT[:], ALU.mult)
            kdT = tmp("g_kdT")
            tt(kdT[:], gk_[par][:, cs], edT[:], ALU.mult)
            p_kd = P()
            tr(p_kd[:], kdT[:], ident)
            yield
            kdg = tmp("g_kd")
            cp(r_(kdg[:]), p_kd[:])
            p_at = P()
            mmr(p_at[:], kppT[:], qpT[:])
            yield
            atg = tmp("g_at")
            tt(r_(atg[:]), p_at[:], m01, ALU.mult)
            o_gla = tmp("o_gla")
            vt = tm[:, 0:128]
            for c in range(2):
                r = slice(64 * c, 64 * c + 64)
                po = P()
                mmr(po[:], qpT[:], S_gla[:], start=True, stop=False)
                mmr(po[:], atg[r, :], vt[r, :], start=False, stop=True)
                yield
                cp(o_gla[r, :], po[r, :])
                p_s = P()
                mmr(p_s[:], kdg[r, :], vt[r, :])
                yield
                stt(r_(S_gla[:]), S_gla[:], egT[:, 64 * c + 63:64 * c + 64], p_s[:], ALU.mult, ALU.add)
            fw.dma("sp", ogla_d[tg:tg + 128, :], o_gla[:])

        def exhaust(g):
            for _ in g:
                pass

        def step(g):
            try:
                next(g)
                return True
            except StopIteration:
                return False

        exhaust(prologue(0))
        tiles = [(st, s) for st in range(NST) for s in range(NSUB)]
        if "dn" in parts:
            exhaust(dn_prep(*tiles[0]))
        for i, (st, s) in enumerate(tiles):
            live = []
            bg = prologue(st + 1) if (s == 0 and st + 1 < NST) else None
            nxt = tiles[i + 1] if i + 1 < len(tiles) else None
            if "dn" in parts:
                live.append(dn_scan(st, s))
            if "ssm" in parts:
                live.append(ssm_tile(st, s))
            if "gla" in parts:
                live.append(gla_tile(st, s))
            if "dn" in parts and nxt is not None:
                live.append(dn_prep(*nxt))
            while live:
                for g in list(live):
                    if not step(g):
                        live.remove(g)
                if bg is not None:
                    if not (step(bg) and step(bg)):
                        bg = None
            if bg is not None:
                exhaust(bg)
        fw.finish()
        print("mixer sbuf bytes/partition", fw.sb_bytes, "instr", fw.cnt)
    return nc


O_DNQ, O_DNK, O_DNV, O_DNB, O_DNA, O_DNG = 0, 1024, 2048, 3072, 3080, 3088
O_SZ, O_SX, O_SB, O_SC, O_SDT = 4112, 5136, 6160, 6416, 6672
O_GQ, O_GK, O_GV, O_GLR, O_GO, O_BR = 6688, 7200, 7712, 8736, 8752, 9776
IN_TOTAL = 15920


def fm_layout(a):
    R, C = a.shape
    return np.ascontiguousarray(a.reshape(R // 128, 128, C).transpose(1, 0, 2))


def mixer_core_inputs(c, L, xT_l, P, consts):
    w_in = P["w_in"][L]
    g = c // 4
    hg, half = c // 2, c % 2
    fm_cols = np.concatenate([
        np.arange(O_DNQ + 128 * c, O_DNQ + 128 * c + 128),
        np.arange(O_DNK + 128 * c, O_DNK + 128 * c + 128),
        np.arange(O_DNV + 128 * c, O_DNV + 128 * c + 128),
        np.arange(O_SX + 128 * c, O_SX + 128 * c + 128),
        np.arange(O_SB + 128 * g, O_SB + 128 * g + 128),
        np.arange(O_SC + 128 * g, O_SC + 128 * g + 128),
        np.arange(O_GQ + 128 * hg, O_GQ + 128 * hg + 128),
        np.arange(O_GK + 128 * hg, O_GK + 128 * hg + 128),
        np.arange(O_GLR, O_GLR + 16)])
    tm_cols = np.concatenate([
        np.arange(O_GV + 256 * hg + 128 * half, O_GV + 256 * hg + 128 * half + 128),
        [O_DNB + c, O_DNA + c, O_SDT + 2 * c, O_SDT + 2 * c + 1]]).astype(np.int64)
    dcw = P["dn_conv_w"][L]
    scw = P["ssm_conv_w"][L]
    scb = P["ssm_conv_b"][L]
    chans = [(dcw, 128 * c), (dcw, 1024 + 128 * c), (dcw, 2048 + 128 * c),
             (scw, 128 * c), (scw, 1024 + 128 * g), (scw, 1280 + 128 * g)]
    cw = np.stack([w[:, o:o + 128].T for (w, o) in chans], axis=1)
    cb = np.stack([scb[o:o + 128] for o in (128 * c, 1024 + 128 * g, 1280 + 128 * g)], axis=1)
    sc = np.array([P["dn_a_log"][L][c], P["dn_dt_bias"][L][c],
                   P["ssm_dt_bias"][L][2 * c], P["ssm_dt_bias"][L][2 * c + 1],
                   P["ssm_a_log"][L][2 * c], P["ssm_a_log"][L][2 * c + 1],
                   P["ssm_d"][L][2 * c], P["ssm_d"][L][2 * c + 1]], np.float32)
    return {
        "xT": xT_l,
        "gain": np.ascontiguousarray(P["pre_mix_norm"][L].reshape(KC, 128).T),
        "wfm": fm_layout(w_in[:, fm_cols]),
        "wtm": fm_layout(w_in[:, tm_cols]),
        "cw": np.ascontiguousarray(cw, dtype=np.float32),
        "cb": np.ascontiguousarray(cb, dtype=np.float32),
        "scal": np.ascontiguousarray(np.broadcast_to(sc[None, :], (128, 8))),
        "w2": np.ascontiguousarray(P["gla_gate_w2"][L][:, 128 * hg:128 * hg + 128]),
        "gb": np.ascontiguousarray(P["gla_gate_b"][L][128 * hg:128 * hg + 128].reshape(128, 1)),
        "consts": consts,
    }


NWT = 512
RING = 8
G_PRE, G_POSTMIX, G_PREMLP, G_POSTMLP, G_PLEPRE, G_PLEPOST = range(6)


def build_dense(TC, NT=512):
    nc = bass.Bass("TRN2", target_bir_lowering=False)
    NTT = TC // NT

    def din(name, shape):
        return nc.dram_tensor(name, list(shape), F32, kind="ExternalInput").ap()

    xT_d = din("xT", [128, KC, TC])
    odn_d = din("odnT", [128, 8, TC])
    ossm_d = din("ossmT", [128, 8, TC])
    ogla_d = din("oglaT", [128, 8, TC])
    pT_d = din("pT", [128, 2, TC])
    ws_d = din("wstream", [NWT, 128, 8, 128])
    wpp_d = din("wpp", [128, 2, D])
    gains_d = din("gains", [128, 6, KC])
    bn_d = din("bnorm", [128, 11])
    const_d = din("consts", [128, NCONST, 128])
    out_d = nc.dram_tensor("xoT", [128, KC, TC], F32, kind="ExternalOutput").ap()

    with ExitStack() as es:
        fw = FW(nc, es)
        sb, mm, act, ts, stt, tt, cp = fw.sb, fw.mm, fw.act, fw.ts, fw.stt, fw.tt, fw.cp
        consts = sb("consts", [128, NCONST, 128])
        gains = sb("gains", [128, 6, KC])
        bn = sb("bn", [128, 11])
        wpp = sb("wpp", [128, 2, D], BF16)
        fw.dma("sp", consts[:], const_d)
        fw.dma("sp", gains[:], gains_d)
        fw.dma("sp", bn[:], bn_d)
        fw.dma("pool", wpp[:], wpp_d)
        ones = consts[:, C_ONES, :]
        ones_bf = sb("ones_bf", [128, 128], BF16)
        cp(ones_bf[:], ones, e="dve")

        xs = sb("xs", [128, KC, NT])
        hT = sb("hT", [128, KC, NT], BF16)
        big = sb("big", [128, KC, NT])
        mp = sb("mp", [128, KC, NT], BF16)
        arena = es.enter_context(nc.sbuf_tensor("s_arena", [128, 32, NT], BF16))
        yb = [TT(arena[:, 8 * b:8 * b + 8, :], "yb%d" % b) for b in range(3)]
        upT = TT(arena[:, :, :], "upT")
        ring = [sb("ring%d" % i, [128, 8, 128], BF16) for i in range(RING)]
        pn = fw.ps("pn")
        pss = fw.ps("pss")
        pacc = [fw.ps("pacc%d" % i) for i in range(6)]
        pi = [0]

        def pbank():
            p = pacc[pi[0] % len(pacc)]
            pi[0] += 1
            return p

        bigsub = [big.sub((slice(None), k, slice(None)), "bigsub%d" % k) for k in range(KC)]
        scr_i = {"yz": [0, bigsub[0:4]], "ot": [0, bigsub[4:8]], "sg": [0, bigsub[8:12]], "t1": [0, bigsub[12:14]]}

        def scr(name):
            ent = scr_i[name]
            t = ent[1][ent[0] % len(ent[1])]
            ent[0] += 1
            return t

        tmp_i = {}

        def tmp(name, shape=(128, NT), n=2, dtype=F32):
            if name not in tmp_i:
                tmp_i[name] = [0, [sb("%s_%d" % (name, i), shape, dtype) for i in range(n)]]
            ent = tmp_i[name]
            t = ent[1][ent[0] % n]
            ent[0] += 1
            return t

        wstate = {"issued": 0, "next": 0, "total": NWT * NTT}

        def get_w():
            idx = wstate["next"]
            while wstate["issued"] < min(idx + RING, wstate["total"]):
                k = wstate["issued"]
                fw.dma("pool", ring[k % RING][:], ws_d[k % NWT])
                wstate["issued"] += 1
            wstate["next"] += 1
            return ring[idx % RING]

        def proj(out_ps, rhs_of_kc, nk):
            w = None
            for kc in range(nk):
                if kc % 8 == 0:
                    w = get_w()
                mm(out_ps, w[:, kc % 8, :], rhs_of_kc(kc), start=(kc == 0), stop=(kc == nk - 1))

        def rms_rstd(blocks, nfeat, dst, f32=False):
            n = len(blocks)
            for i, v in enumerate(blocks):
                if f32:
                    sq = tmp("sqf")
                    act(sq[:], v, AF.Square)
                    mm(pn[:, 0:NT], ones, sq[:], start=(i == 0), stop=(i == n - 1))
                else:
                    sq = tmp("sqb", dtype=BF16)
                    act(sq[:], v, AF.Square)
                    mm(pn[:, 0:NT], ones_bf[:], sq[:], start=(i == 0), stop=(i == n - 1))
            lt = tmp("lnt", n=1)
            act(lt[:], pn[:, 0:NT], AF.Ln, bias=EPS, scale=1.0 / nfeat)
            act(dst[:], lt[:], AF.Exp, scale=-0.5)

        def norm_to_hT(gi):
            rs = tmp("rstd", n=1)
            rms_rstd([xs[:, kc, :] for kc in range(KC)], D, rs)
            for kc in range(KC):
                stt(hT[:, kc, :], xs[:, kc, :], gains[:, gi, kc:kc + 1], rs[:], ALU.mult, ALU.mult)

        def residual_add(gi):
            rs = tmp("rstd", n=1)
            rms_rstd([big[:, kc, :] for kc in range(KC)], D, rs)
            for kc in range(KC):
                t1 = tmp("resid", n=1)
                stt(t1[:], big[:, kc, :], gains[:, gi, kc:kc + 1], rs[:], ALU.mult, ALU.mult)
                tt(xs[:, kc, :], xs[:, kc, :], t1[:], ALU.add)

        for ti in range(NTT):
            t0 = ti * NT
            tsl = slice(t0, t0 + NT)
            fw.dma("sp", xs[:], xT_d[:, :, tsl])
            norm_to_hT(G_PRE)
            if ti > 0:
                fw.alias_barrier(yb, [upT])
                fw.alias_barrier(bigsub, [big])
            for blk in range(8):
                ot = scr("ot")
                fw.dma("sp", ot[:], odn_d[:, blk, tsl])
                sq = tmp("sqf")
                act(sq[:], ot[:], AF.Square)
                mm(pss[:, 0:NT], ones, sq[:])
                lt = tmp("lnt2", n=1)
                act(lt[:], pss[:, 0:NT], AF.Ln, bias=EPS, scale=1.0 / 128)
                rs = tmp("rs2", n=1)
                act(rs[:], lt[:], AF.Exp, scale=-0.5)
                pg = pbank()
                proj(pg[:, 0:NT], lambda kc: hT[:, kc, :], KC)
                sg = scr("sg")
                act(sg[:], pg[:, 0:NT], AF.Silu)
                t1 = scr("t1")
                stt(t1[:], ot[:], bn[:, 0:1], rs[:], ALU.mult, ALU.mult)
                tt(yb[0][:, blk, :], t1[:], sg[:], ALU.mult)
            for grp in range(2):
                yz = []
                for j in range(4):
                    blk = 4 * grp + j
                    ot = scr("ot")
                    fw.dma("sp", ot[:], ossm_d[:, blk, tsl])
                    pg = pbank()
                    proj(pg[:, 0:NT], lambda kc: hT[:, kc, :], KC)
                    sg = scr("sg")
                    act(sg[:], pg[:, 0:NT], AF.Silu)
                    y = scr("yz")
                    tt(y[:], ot[:], sg[:], ALU.mult)
                    sq = tmp("sqf")
                    act(sq[:], y[:], AF.Square)
                    mm(pss[:, 0:NT], ones, sq[:], start=(j == 0), stop=(j == 3))
                    yz.append(y)
                lt = tmp("lnt2", n=1)
                act(lt[:], pss[:, 0:NT], AF.Ln, bias=EPS, scale=1.0 / 512)
                rs = tmp("rs2", n=1)
                act(rs[:], lt[:], AF.Exp, scale=-0.5)
                for j in range(4):
                    blk = 4 * grp + j
                    stt(yb[1][:, blk, :], yz[j][:], bn[:, 1 + blk:2 + blk], rs[:], ALU.mult, ALU.mult)
            for hd in range(4):
                ots, sgs = [], []
                for j in range(2):
                    blk = 2 * hd + j
                    ot = scr("ot")
                    fw.dma("sp", ot[:], ogla_d[:, blk, tsl])
                    sq = tmp("sqf")
                    act(sq[:], ot[:], AF.Square)
                    mm(pss[:, 0:NT], ones, sq[:], start=(j == 0), stop=(j == 1))
                    pg = pbank()
                    proj(pg[:, 0:NT], lambda kc: hT[:, kc, :], KC)
                    sg = scr("sg")
                    act(sg[:], pg[:, 0:NT], AF.Silu)
                    ots.append(ot)
                    sgs.append(sg)
                lt = tmp("lnt2", n=1)
                act(lt[:], pss[:, 0:NT], AF.Ln, bias=EPS, scale=1.0 / 256)
                rs = tmp("rs2", n=1)
                act(rs[:], lt[:], AF.Exp, scale=-0.5)
                for j in range(2):
                    blk = 2 * hd + j
                    t1 = scr("t1")
                    stt(t1[:], ots[j][:], bn[:, 9 + j:10 + j], rs[:], ALU.mult, ALU.mult)
                    tt(yb[2][:, blk, :], t1[:], sgs[j][:], ALU.mult)
            for dblk in range(KC):
                macc = tmp("macc")
                for b in range(3):
                    pg = pbank()
                    proj(pg[:, 0:NT], lambda kc: hT[:, kc, :], KC)
                    sgm = tmp("sgm", n=2)
                    act(sgm[:], pg[:, 0:NT], AF.Sigmoid)
                    pu = pbank()
                    proj(pu[:, 0:NT], lambda kc, b=b: yb[b][:, kc, :], 8)
                    if b == 0:
                        tt(macc[:], pu[:, 0:NT], sgm[:], ALU.mult)
                    else:
                        t2 = tmp("t2", n=1)
                        tt(t2[:], pu[:, 0:NT], sgm[:], ALU.mult)
                        if b == 1:
                            tt(macc[:], macc[:], t2[:], ALU.add)
                        else:
                            tt(mp[:, dblk, :], macc[:], t2[:], ALU.add)
            fw.alias_barrier([big], bigsub)
            for dblk in range(KC):
                po = pbank()
                proj(po[:, 0:NT], lambda kc: mp[:, kc, :], KC)
                cp(big[:, dblk, :], po[:, 0:NT])
            residual_add(G_POSTMIX)
            norm_to_hT(G_PREMLP)
            fw.alias_barrier([upT], yb)
            for half in range(2):
                for fb in range(32):
                    pu = pbank()
                    proj(pu[:, 0:NT], lambda kc: hT[:, kc, :], KC)
                    r = tmp("relu")
                    act(r[:], pu[:, 0:NT], AF.Relu)
                    tt(upT[:, fb, :], r[:], r[:], ALU.mult)
                for dblk in range(KC):
                    pd = pbank()
                    proj(pd[:, 0:NT], lambda kc: upT[:, kc, :], 32)
                    if half == 0:
                        cp(big[:, dblk, :], pd[:, 0:NT])
                    else:
                        tt(big[:, dblk, :], pd[:, 0:NT], big[:, dblk, :], ALU.add)
            residual_add(G_POSTMLP)
            norm_to_hT(G_PLEPRE)
            pf = tmp("pf", (128, 2, NT), n=1)
            fw.dma("sp", pf[:], pT_d[:, :, tsl])
            pb = tmp("pb", (128, 2, NT), n=1, dtype=BF16)
            cp(pb[:], pf[:], e="dve")
            for dblk in range(KC):
                pg = pbank()
                proj(pg[:, 0:NT], lambda kc: hT[:, kc, :], KC)
                sgm = tmp("sgm", n=2)
                act(sgm[:], pg[:, 0:NT], AF.Sigmoid)
                pe_ = pbank()
                for c in range(2):
                    mm(pe_[:, 0:NT], wpp[:, c, dblk * 128:dblk * 128 + 128], pb[:, c, :], start=(c == 0), stop=(c == 1))
                tt(big[:, dblk, :], pe_[:, 0:NT], sgm[:], ALU.mult)
            residual_add(G_PLEPOST)
            fw.dma("sp", out_d[:, :, tsl], xs[:])
        assert wstate["next"] == wstate["total"], (wstate, NWT)
        fw.finish()
        print("dense sbuf bytes/partition", fw.sb_bytes + 32 * NT * 2, "instr", fw.cnt)
    return nc


def half_tile(W, r0, c0):
    return W[r0:r0 + 1024, c0:c0 + 128].reshape(8, 128, 128).transpose(1, 0, 2)


def build_wstream(P, L):
    w_in, wb, wo = P["w_in"][L], P["w_branch"][L], P["w_out"][L]
    wu, wd, wg = P["w_up"][L], P["w_down"][L], P["w_ple_gate"][L]
    ws = np.empty((NWT, 128, 8, 128), np.float32)
    i = 0

    def put(W, r0, c0):
        nonlocal i
        ws[i] = half_tile(W, r0, c0)
        i += 1

    for off in (O_DNG, O_SZ, O_GO):
        for blk in range(8):
            put(w_in, 0, off + blk * 128)
            put(w_in, 1024, off + blk * 128)
    for dblk in range(16):
        for b in range(3):
            put(w_in, 0, O_BR + b * D + dblk * 128)
            put(w_in, 1024, O_BR + b * D + dblk * 128)
            put(wb[b], 0, dblk * 128)
    for dblk in range(16):
        put(wo, 0, dblk * 128)
        put(wo, 1024, dblk * 128)
    for half in range(2):
        for fb in range(32):
            put(wu, 0, (half * 32 + fb) * 128)
            put(wu, 1024, (half * 32 + fb) * 128)
        for dblk in range(16):
            for q in range(4):
                put(wd, half * 4096 + q * 1024, dblk * 128)
    for dblk in range(16):
        put(wg, 0, dblk * 128)
        put(wg, 1024, dblk * 128)
    assert i == NWT, i
    return ws


def tok_to_fm(a):
    Tn, C = a.shape
    return np.ascontiguousarray(a.reshape(Tn, C // 128, 128).transpose(2, 1, 0))


def fm_to_tok(a):
    p, nb, Tn = a.shape
    return np.ascontiguousarray(a.transpose(2, 1, 0).reshape(Tn, nb * 128))


def dense_shared_inputs(L, P, consts):
    gains = np.stack([P[k][L].reshape(KC, 128).T for k in
                      ("pre_mix_norm", "post_mix_norm", "pre_mlp_norm", "post_mlp_norm", "ple_pre_norm", "ple_post_norm")], axis=1)
    bn = np.concatenate([P["dn_norm"][L].reshape(128, 1), P["ssm_norm"][L].reshape(8, 128).T,
                         P["gla_norm"][L].reshape(2, 128).T], axis=1)
    return {
        "wstream": build_wstream(P, L),
        "wpp": fm_layout(P["w_ple_proj"][L]),
        "gains": np.ascontiguousarray(gains, dtype=np.float32),
        "bnorm": np.ascontiguousarray(bn, dtype=np.float32),
        "consts": consts,
    }


SEQ = 8192
NCORE = 8
TCORE = SEQ // NCORE
_PROG = {}


def _prog(kind):
    if kind not in _PROG:
        _PROG[kind] = build_mixer(SEQ) if kind == "mixer" else build_dense(TCORE)
    return _PROG[kind]


def kernel(**inputs):
    P = {k: np.asarray(v) for k, v in inputs.items()}
    x = np.ascontiguousarray(P["x"][0], dtype=np.float32)
    consts = make_consts()
    cores = list(range(NCORE))
    for L in range(2):
        xT_l = tok_to_fm(x)
        in_maps = [mixer_core_inputs(c, L, xT_l, P, consts) for c in cores]
        res = run_bass_kernel_spmd(_prog("mixer"), in_maps, core_ids=cores).results
        o_dn = np.concatenate([res[c]["o_dn"] for c in cores], axis=1)
        o_ssm = np.concatenate([res[c]["o_ssm"] for c in cores], axis=1)
        o_gla = np.concatenate([res[c]["o_gla"] for c in cores], axis=1)
        del res, in_maps
        sh = dense_shared_inputs(L, P, consts)
        in_maps = []
        for c in cores:
            ts_ = slice(c * TCORE, (c + 1) * TCORE)
            im = dict(sh)
            im.update({"xT": tok_to_fm(x[ts_]), "odnT": tok_to_fm(o_dn[ts_]), "ossmT": tok_to_fm(o_ssm[ts_]),
                       "oglaT": tok_to_fm(o_gla[ts_]), "pT": tok_to_fm(np.ascontiguousarray(P["p"][L][0, ts_]))})
            in_maps.append(im)
        res = run_bass_kernel_spmd(_prog("dense"), in_maps, core_ids=cores).results
        x = np.concatenate([fm_to_tok(res[c]["xoT"]) for c in cores], axis=0)
        del res, in_maps, sh
    return x[None].astype(np.float32)
```
